# Optimizing a Trainium2 kernel written in Bass

```python
import math
import jax, jax.numpy as jnp
from jax import lax
import numpy as np


D_MODEL = 2048
BATCH = 8
SEQ = 4096
DEPTH = 1
DEC_BATCH = 16
DEC_SEQ = 2048
PAST_LEN = 128

MIX_WIDTH = D_MODEL
HEAD_DIM = 128
ATTN_WIDTH = MIX_WIDTH // 2
N_HEADS = ATTN_WIDTH // HEAD_DIM
N_KV = 2
GROUP = N_HEADS // N_KV
KV_WIDTH = N_KV * HEAD_DIM
WINDOW = 128
BLOCK = 128
ROPE_THETA = 10000.0
GMLP_WIDTH = MIX_WIDTH - ATTN_WIDTH
GMLP_HEAD_DIM = 128
N_GMLP_HEADS = GMLP_WIDTH // GMLP_HEAD_DIM
CHUNK = 128
PLE_DIM = 256
EPS = 1e-6
IN_SPLITS = (
    ATTN_WIDTH,
    ATTN_WIDTH + KV_WIDTH,
    ATTN_WIDTH + 2 * KV_WIDTH,
    2 * ATTN_WIDTH + 2 * KV_WIDTH,
    2 * ATTN_WIDTH + 2 * KV_WIDTH + GMLP_WIDTH,
    2 * ATTN_WIDTH + 2 * KV_WIDTH + 2 * GMLP_WIDTH,
)
IN_WIDTH = 2 * ATTN_WIDTH + 2 * KV_WIDTH + 3 * GMLP_WIDTH

kernel_name = "hymba_window_gqa_gmlp_sandwich_ple_encoder"


def rms_norm(x, g):
    xf = x.astype(jnp.float32)
    y = xf * lax.rsqrt(jnp.mean(xf * xf, axis=-1, keepdims=True) + EPS)
    return (y * g.astype(jnp.float32)).astype(x.dtype)


def rope(x):
    S = x.shape[1]
    inv = 1.0 / (ROPE_THETA ** (jnp.arange(0, HEAD_DIM, 2, dtype=jnp.float32) / HEAD_DIM))
    ang = jnp.arange(S, dtype=jnp.float32)[:, None] * inv[None, :]
    cos = jnp.cos(ang)[None, :, None, :]
    sin = jnp.sin(ang)[None, :, None, :]
    xf = x.astype(jnp.float32)
    x1, x2 = jnp.split(xf, 2, axis=-1)
    out = jnp.concatenate([x1 * cos - x2 * sin, x2 * cos + x1 * sin], axis=-1)
    return out.astype(x.dtype)


def window_attention(q, k, v, sink):
    B, S, H, D = q.shape
    nb = S // BLOCK
    qb = q.reshape(B, nb, BLOCK, N_KV, GROUP, D)
    pad = ((0, 0), (BLOCK, BLOCK), (0, 0), (0, 0))

    def band(t):
        tb = jnp.pad(t, pad).reshape(B, nb + 2, BLOCK, N_KV, D)
        return jnp.concatenate([tb[:, :-2], tb[:, 1:-1], tb[:, 2:]], axis=2)

    kb, vb = band(k), band(v)
    s = jnp.einsum('bnqgrd,bnkgd->bngrqk', qb, kb).astype(jnp.float32) * (D ** -0.5)
    qpos = jnp.arange(nb)[:, None, None] * BLOCK + jnp.arange(BLOCK)[None, :, None]
    kpos = jnp.arange(nb)[:, None, None] * BLOCK - BLOCK + jnp.arange(3 * BLOCK)[None, None, :]
    valid = (jnp.abs(qpos - kpos) <= WINDOW) & (kpos >= 0) & (kpos < S)
    s = jnp.where(valid[None, :, None, None], s, -1e30)
    sink_logit = jnp.broadcast_to(
        sink.astype(jnp.float32).reshape(1, 1, N_KV, GROUP, 1, 1), s.shape[:-1] + (1,))
    probs = jax.nn.softmax(jnp.concatenate([s, sink_logit], axis=-1), axis=-1)[..., :-1]
    o = jnp.einsum('bngrqk,bnkgd->bnqgrd', probs.astype(v.dtype), vb)
    return o.reshape(B, S, H * D)


def spatial_gating(u, v, ln_g, ln_b, ws, bs):
    B, S, _ = v.shape
    nc = S // CHUNK
    vf = v.astype(jnp.float32)
    mu = jnp.mean(vf, axis=-1, keepdims=True)
    var = jnp.mean(jnp.square(vf - mu), axis=-1, keepdims=True)
    vn = ((vf - mu) * lax.rsqrt(var + EPS) * ln_g.astype(jnp.float32)
          + ln_b.astype(jnp.float32)).astype(v.dtype)
    vc = vn.reshape(B, nc, CHUNK, N_GMLP_HEADS, GMLP_HEAD_DIM)
    mixed = jnp.einsum('hpq,bnqhc->bnphc', ws, vc) + bs.T[None, None, :, :, None]
    return u * mixed.reshape(B, S, GMLP_WIDTH)


def layer(x, p, pre_g, w_in, sink, ln_g, ln_b, ws, bs, w_out, post_g, w_pe, w_pg):
    B, S, _ = x.shape
    h = rms_norm(x, pre_g)
    z = h @ w_in
    q, k, v, g_attn, u, vg, g_gmlp = jnp.split(z, IN_SPLITS, axis=-1)
    q = rope(q.reshape(B, S, N_HEADS, HEAD_DIM))
    k = rope(k.reshape(B, S, N_KV, HEAD_DIM))
    v = v.reshape(B, S, N_KV, HEAD_DIM)
    a = window_attention(q, k, v, sink) * jax.nn.silu(g_attn)
    m = spatial_gating(jax.nn.gelu(u, approximate=False), jax.nn.gelu(vg, approximate=False),
                       ln_g, ln_b, ws, bs) * jax.nn.silu(g_gmlp)
    y = jnp.concatenate([a, m], axis=-1) @ w_out
    x = x + rms_norm(y, post_g)
    x = x + jax.nn.sigmoid(x @ w_pg) * (p @ w_pe)
    return x


def setup_inputs(seed: int = 0) -> dict:
    key = jax.random.key(seed)
    ks = jax.random.split(key, 16)
    f = jnp.float32
    nrm = jax.random.normal
    return {
        "x_prompt": nrm(ks[0], (BATCH, SEQ, D_MODEL), f),
        "x_sample": nrm(ks[1], (DEC_BATCH, DEC_SEQ, D_MODEL), f),
        "p_prompt": nrm(ks[2], (DEPTH, BATCH, SEQ, PLE_DIM), f),
        "p_sample": nrm(ks[3], (DEPTH, DEC_BATCH, DEC_SEQ, PLE_DIM), f),
        "pre_norm_g": 1.0 + 0.05 * nrm(ks[4], (DEPTH, D_MODEL), f),
        "w_in": nrm(ks[5], (DEPTH, D_MODEL, IN_WIDTH), f) * D_MODEL ** -0.5,
        "attn_sink": 0.5 * nrm(ks[6], (DEPTH, N_HEADS), f),
        "gmlp_ln_g": 1.0 + 0.05 * nrm(ks[7], (DEPTH, GMLP_WIDTH), f),
        "gmlp_ln_b": 0.02 * nrm(ks[8], (DEPTH, GMLP_WIDTH), f),
        "gmlp_ws": nrm(ks[9], (DEPTH, N_GMLP_HEADS, CHUNK, CHUNK), f) * CHUNK ** -0.5,
        "gmlp_bs": 1.0 + 0.05 * nrm(ks[10], (DEPTH, N_GMLP_HEADS, CHUNK), f),
        "w_out": nrm(ks[11], (DEPTH, MIX_WIDTH, D_MODEL), f) * MIX_WIDTH ** -0.5,
        "post_norm_g": 1.0 + 0.05 * nrm(ks[12], (DEPTH, D_MODEL), f),
        "w_pe": nrm(ks[13], (DEPTH, PLE_DIM, D_MODEL), f) * PLE_DIM ** -0.5,
        "w_pg": nrm(ks[14], (DEPTH, D_MODEL, D_MODEL), f) * D_MODEL ** -0.5,
    }


def reference(x_prompt, x_sample, p_prompt, p_sample, pre_norm_g, w_in, attn_sink,
              gmlp_ln_g, gmlp_ln_b, gmlp_ws, gmlp_bs, w_out, post_norm_g, w_pe, w_pg):
    y_prompt = x_prompt
    y_sample = x_sample
    for i in range(DEPTH):
        params = (pre_norm_g[i], w_in[i], attn_sink[i], gmlp_ln_g[i], gmlp_ln_b[i],
                  gmlp_ws[i], gmlp_bs[i], w_out[i], post_norm_g[i], w_pe[i], w_pg[i])
        y_prompt = layer(y_prompt, p_prompt[i], *params)
        y_sample = layer(y_sample, p_sample[i], *params)
    return (y_prompt, y_sample)
```

```python
import math
from contextlib import ExitStack

import numpy as np
import concourse.bass as bass
import concourse.mybir as mybir
from concourse.bass_utils import run_bass_kernel_spmd

F32 = mybir.dt.float32
BF16 = mybir.dt.bfloat16
AF = mybir.ActivationFunctionType
ALU = mybir.AluOpType
AX = mybir.AxisListType

D = 2048
INW = 5632
NB = 4
EPS = 1e-6
SCALE = 128.0 ** -0.5


class _Op:
    __slots__ = ("eng", "fn", "deps", "dma", "key", "sig", "cnt")

    def __init__(self, eng, fn, deps, key):
        self.eng = eng
        self.fn = fn
        self.deps = deps
        self.dma = key is not None
        self.key = key
        self.sig = False
        self.cnt = 0


class Sched:
    ENGS = ("pe", "act", "dve", "pool", "sp")

    def __init__(self):
        self.ops = []
        self.lastw = {}
        self.readers = {}
        self.group_keys = set()

    def add(self, eng, fn, reads=(), writes=(), dma_key=None):
        idx = len(self.ops)
        deps = set()
        for r in reads:
            w = self.lastw.get(r)
            if w is not None:
                deps.add(w)
        for w_ in writes:
            w = self.lastw.get(w_)
            if w is not None:
                deps.add(w)
            rl = self.readers.get(w_)
            if rl:
                deps.update(rl)
        for r in reads:
            self.readers.setdefault(r, []).append(idx)
        for w_ in writes:
            self.lastw[w_] = idx
            self.readers[w_] = []
        deps.discard(idx)
        self.ops.append(_Op(eng, fn, deps, dma_key))
        return idx

    def emit(self, nc, stack):
        ops = self.ops
        for op in ops:
            nd = set()
            for d in op.deps:
                p = ops[d]
                if (not p.dma) and p.eng == op.eng and p.eng in ("pe", "sp"):
                    continue
                if p.dma and op.dma and p.key == op.key and p.key in self.group_keys:
                    continue
                nd.add(d)
            op.deps = nd
            for d in nd:
                ops[d].sig = True
        esem = {e: stack.enter_context(nc.semaphore("s_" + e)) for e in self.ENGS}
        dsem, dcount = {}, {}
        ecount = {e: 0 for e in self.ENGS}
        for op in ops:
            if op.dma:
                op.sig = True
                if op.key not in dsem:
                    dsem[op.key] = stack.enter_context(nc.semaphore("d%d" % len(dsem)))
                    dcount[op.key] = 0
                dcount[op.key] += 16
                op.cnt = dcount[op.key]
            elif op.sig:
                ecount[op.eng] += 1
                op.cnt = ecount[op.eng]
        per_eng = {e: [] for e in self.ENGS}
        for op in ops:
            per_eng[op.eng].append(op)
        block = stack.enter_context(nc.Block())
        group_keys = self.group_keys

        def run(e, engine):
            waited = {}
            for op in per_eng[e]:
                need = {}
                for d in op.deps:
                    p = ops[d]
                    if p.dma:
                        s = dsem[p.key]
                        v = dcount[p.key] if p.key in group_keys else p.cnt
                    else:
                        s = esem[p.eng]
                        v = p.cnt
                    k = id(s)
                    if k not in need or need[k][1] < v:
                        need[k] = (s, v)
                for k, (s, v) in need.items():
                    if waited.get(k, 0) >= v:
                        continue
                    waited[k] = v
                    engine.wait_ge(s, v)
                ins = op.fn(engine)
                if op.sig:
                    if op.dma:
                        ins.then_inc(dsem[op.key], 16)
                    else:
                        ins.then_inc(esem[op.eng], 1)
            if e == "sp":
                for k, s in dsem.items():
                    engine.wait_ge(s, dcount[k])

        @block.tensor
        def _(eng):
            run("pe", eng)

        @block.scalar
        def _(eng):
            run("act", eng)

        @block.vector
        def _(eng):
            run("dve", eng)

        @block.gpsimd
        def _(eng):
            run("pool", eng)

        @block.sync
        def _(eng):
            run("sp", eng)


def build(seqs, rows_a, rows_b, max_ops=None, marks=None):
    nc = bass.Bass("TRN2", target_bir_lowering=False)

    def din(name, shape):
        return nc.dram_tensor(name, shape, F32, kind="ExternalInput").ap()

    xin = {"a": din("xa", [rows_a, D]), "b": din("xb", [rows_b, D])}
    pin = {"a": din("pa", [rows_a, 256]), "b": din("pb", [rows_b, 256])}
    yout = {"a": nc.dram_tensor("ya", [rows_a, D], F32, kind="ExternalOutput").ap(),
            "b": nc.dram_tensor("yb", [rows_b, D], F32, kind="ExternalOutput").ap()}
    w_in = din("w_in", [D, INW])
    w_out = din("w_out", [D, D])
    w_pg = din("w_pg", [D, D])
    w_pe = din("w_pe", [256, D])
    pre_g = din("pre_g", [1, D])
    post_g = din("post_g", [1, D])
    sink = din("sink", [1, 8])
    ln_g = din("ln_g", [1, 1024])
    ln_b = din("ln_b", [1, 1024])
    ws_d = din("ws", [8, 128, 128])
    bs_d = din("bs", [1, 1024])
    cs_d = din("cs", [4096, 128])
    win_s = nc.dram_tensor("win_s", [11, 128, 8192], BF16).ap()
    wout_s = nc.dram_tensor("wout_s", [4, 128, 8192], BF16).ap()
    wpg_s = nc.dram_tensor("wpg_s", [4, 128, 8192], BF16).ap()
    wpe_s = nc.dram_tensor("wpe_s", [4, 128, 1024], BF16).ap()

    S = Sched()
    S.group_keys.add("setup")
    with ExitStack() as st:
        st.enter_context(nc.allow_non_contiguous_dma(reason="tiny per-partition parameter loads"))

        def sb(name, shape, dt):
            return st.enter_context(nc.sbuf_tensor("sb_" + name, shape, dt))

        RBt = sb("RB", [128, 12288], BF16)
        hT = RBt[:, :].rearrange("p (k t) -> p k t", k=16)
        hb = sb("hb", [128, 2, 2048], BF16)
        _rbf = RBt[:, :].bitcast(F32)
        xrl = [_rbf[:, 0:2048], _rbf[:, 2048:4096], _rbf[:, 4096:6144],
               hb[:, :, :].rearrange("p s c -> p (s c)").bitcast(F32)]
        x1b = sb("x1b", [128, 2, 2048], BF16)
        XHt = sb("XH", [128, 4096], F32)
        xh = XHt[:, :].rearrange("p (s c) -> p s c", s=2)
        gv = XHt[:, :].rearrange("p (b c) -> p b c", b=4)
        amT = XHt[:, :].bitcast(BF16).rearrange("p (k t) -> p k t", k=16)
        ys = sb("ys", [128, 4, 2048], F32)
        QTt = sb("QT", [128, 4096], BF16)
        qT = QTt[:, :].rearrange("p (h t) -> p h t", h=8)
        pf = QTt[:, 0:2048].bitcast(F32).rearrange("p (b c) -> p b c", b=4)
        pbb = QTt[:, 2048:3072].rearrange("p (b c) -> p b c", b=4)
        pT = QTt[:, 3072:4096].rearrange("p (c t) -> p c t", c=2)
        kT = sb("kT", [128, 2, 768], BF16)
        vv = sb("vv", [128, 6, 256], BF16)
        GUt = sb("GU", [128, 8192], BF16)
        gaT = GUt[:, 0:4096].rearrange("p (h t) -> p h t", h=8)
        uT = GUt[:, 4096:8192].rearrange("p (h t) -> p h t", h=8)
        x1T = GUt[:, :].rearrange("p (k t) -> p k t", k=16)
        NNt = sb("NN", [128, 4096], BF16)
        nn = NNt[:, :].rearrange("p (b c) -> p b c", b=4)
        sig = NNt[:, 0:2048].bitcast(F32).rearrange("p (s c) -> p s c", s=2)
        tmp2 = NNt[:, 2048:4096].bitcast(F32).rearrange("p (s c) -> p s c", s=2)
        PT = sb("PT", [128, 2, 3, 512], BF16)
        RTt = sb("RT", [128, 2048], F32)
        Rr = RTt[:, 0:1024].rearrange("p (s c) -> p s c", s=2)
        tmpA = RTt[:, 1024:2048].rearrange("p (s c) -> p s c", s=2)
        g_bc = RTt[:, :]
        sg = sb("sg", [128, 2, 512], F32)
        ropeA = sb("ropeA", [128, 512], F32)
        ropeB = sb("ropeB", [128, 512], F32)
        qr = sb("qr", [128, 2, 512], BF16)
        junk = sb("junk", [128, 1024], BF16)
        postg = sb("postg", [128, 2048], F32)
        cs = sb("cs", [128, 6, 128], F32)
        Bias = sb("Bias", [128, 8, 128], F32)
        wsT = sb("wsT", [128, 8, 128], BF16)
        ident = sb("ident", [128, 128], BF16)
        ones = sb("ones", [128, 128], BF16)
        maskP = sb("maskP", [128, 128], BF16)
        maskN = sb("maskN", [128, 128], BF16)
        mf = sb("mf", [128, 128], F32)
        esink = sb("esink", [128, 8], F32)
        lng = sb("lng", [128, 8], F32)
        lnb = sb("lnb", [128, 8], F32)
        gk = sb("gk", [128, 16], F32)
        stt = sb("stt", [128, 64], F32)
        negh = sb("negh", [128, 4], F32)
        wr = sb("wr", [128, 2, 8192], BF16)
        wpe = sb("wpe", [128, 2, 1024], BF16)
        psum = [st.enter_context(nc.psum_tensor("ps%d" % i, [128, 512], F32)) for i in range(8)]
        psb = [p[:, :].bitcast(BF16) for p in psum]
        bank_ctr = [0]

        def bank():
            k = bank_ctr[0] % 8
            bank_ctr[0] += 1
            return k

        phase_cur = {}

        def phase(tok, ph):
            if phase_cur.get(tok) != ph:
                phase_cur[tok] = ph
                return [], [tok]
            return [tok], []

        C_SSQX, C_RX = 0, 6
        C_GSUM, C_GSSQ, C_GMEAN, C_GMSQ, C_GR = 12, 20, 24, 28, 32
        C_YSSQ, C_YT, C_YR = 36, 52, 56
        C_EPS = 60
        C_NH = 61

        def col(c):
            return stt[:, c:c + 1]

        def setup_dma(out, in_, w):
            S.add("sp", lambda e: e.dma_start(out=out, in_=in_), writes=[w], dma_key="setup")

        setup_dma(postg[:, :], post_g.partition_broadcast(128), "postg")
        setup_dma(esink[:, :], sink.partition_broadcast(128), "esink")
        bs_bc = tmpA[:, :, :].rearrange("p s c -> p (s c)")
        setup_dma(bs_bc, bs_d.partition_broadcast(128), ("tmpA", 0))
        for kc in range(16):
            setup_dma(gk[:, kc:kc + 1], pre_g[0:1, kc * 128:(kc + 1) * 128].rearrange("o p -> p o"), "gk")
        for h in range(8):
            setup_dma(lng[:, h:h + 1], ln_g[0:1, h * 128:(h + 1) * 128].rearrange("o p -> p o"), "lng")
            setup_dma(lnb[:, h:h + 1], ln_b[0:1, h * 128:(h + 1) * 128].rearrange("o p -> p o"), "lnb")
        wsf = sg[:, :, :].rearrange("p s (h q) -> p (s h) q", h=4)
        setup_dma(wsf, ws_d.rearrange("h p q -> p h q"), ("sg", 0))
        S.add("pool", lambda e: e.memset(mf[:, :], 1.0), writes=["mf"])
        S.add("pool", lambda e: e.affine_select(out=mf[:, :], in_=mf[:, :], pattern=[[-1, 128]], compare_op=ALU.is_equal,
                                                fill=0.0, base=0, channel_multiplier=1), reads=["mf"], writes=["mf"])
        S.add("pool", lambda e: e.tensor_copy(out=ident[:, :], in_=mf[:, :]), reads=["mf"], writes=["ident"])
        S.add("pool", lambda e: e.memset(mf[:, :], 1.0), reads=["mf"], writes=["mf"])
        S.add("pool", lambda e: e.affine_select(out=mf[:, :], in_=mf[:, :], pattern=[[-1, 128]], compare_op=ALU.is_ge,
                                                fill=0.0, base=0, channel_multiplier=1), reads=["mf"], writes=["mf"])
        S.add("pool", lambda e: e.tensor_copy(out=maskP[:, :], in_=mf[:, :]), reads=["mf"], writes=["maskP"])
        S.add("pool", lambda e: e.memset(mf[:, :], 1.0), reads=["mf"], writes=["mf"])
        S.add("pool", lambda e: e.affine_select(out=mf[:, :], in_=mf[:, :], pattern=[[1, 128]], compare_op=ALU.is_ge,
                                                fill=0.0, base=0, channel_multiplier=-1), reads=["mf"], writes=["mf"])
        S.add("pool", lambda e: e.tensor_copy(out=maskN[:, :], in_=mf[:, :]), reads=["mf"], writes=["maskN"])
        S.add("pool", lambda e: e.memset(ones[:, :], 1.0), writes=["ones"])
        S.add("pool", lambda e: e.memset(stt[:, C_EPS:C_EPS + 1], EPS), writes=["epsc"])
        S.add("pool", lambda e: e.memset(negh[:, :], -0.5), writes=["negh"])
        S.add("act", lambda e: e.activation(out=esink[:, :], in_=esink[:, :], func=AF.Exp), reads=["esink"], writes=["esink"])
        wsb = qr[:, :, :].rearrange("p s (h q) -> p (s h) q", h=4)
        S.add("dve", lambda e: e.tensor_copy(out=wsb, in_=wsf), reads=[("sg", 0)], writes=[("qr", 0)])
        k0 = bank()

        def f_wsT(e):
            for h in range(8):
                ins = e.transpose(out=psb[k0][:, h * 128:(h + 1) * 128], in_=wsb[:, h, :], identity=ident[:, :])
            return ins
        S.add("pe", f_wsT, reads=[("qr", 0), "ident"], writes=[("ps", k0)])
        S.add("act", lambda e: e.copy(out=wsT[:, :, :].rearrange("p h q -> p (h q)"), in_=psb[k0][:, :]),
              reads=[("ps", k0)], writes=["wsT"])
        for half in range(2):
            kk = bank()
            S.add("pe", lambda e, kk=kk, half=half: e.matmul(
                psum[kk][:, :], lhsT=ones[:, :], rhs=wsT[:, 4 * half:4 * half + 4, :].rearrange("p h q -> p (h q)"), start=True, stop=True),
                reads=["ones", "wsT"], writes=[("ps", kk)])
            for h4 in range(4):
                h = 4 * half + h4
                S.add("dve", lambda e, kk=kk, h=h, h4=h4: e.scalar_tensor_tensor(
                    out=Bias[:, h, :], in0=psum[kk][:, h4 * 128:(h4 + 1) * 128], scalar=lnb[:, h:h + 1],
                    in1=bs_bc[:, h * 128:(h + 1) * 128], op0=ALU.mult, op1=ALU.add),
                    reads=[("ps", kk), "lnb", ("tmpA", 0)], writes=[("Bias", h)])

        def mark(nm):
            if marks is not None:
                marks.append((nm, len(S.ops)))
        mark("setup_done")
        w_in_v = w_in.rearrange("(kc p) n -> p kc n", p=128)
        w_out_v = w_out.rearrange("(kc p) n -> p kc n", p=128)
        w_pg_v = w_pg.rearrange("(kc p) n -> p kc n", p=128)
        w_pe_v = w_pe.rearrange("(kc p) n -> p kc n", p=128)
        stF = [ys[:, 0, :], ys[:, 1, :], ys[:, 2, :]]
        _yb = ys[:, 3, :].bitcast(BF16)
        stB = [_yb[:, 0:2048], _yb[:, 2048:4096]]
        cj = [0, 0]
        conv_tok = {}

        def cv_engine_op(eng, o, i_, sc):
            if sc is None:
                if eng == "act":
                    return lambda e: e.copy(out=o, in_=i_)
                return lambda e: e.tensor_copy(out=o, in_=i_)

            def f(e):
                for kc in range(4):
                    oo, ii = o[:, kc * 512:(kc + 1) * 512], i_[:, kc * 512:(kc + 1) * 512]
                    if eng == "act":
                        ins = e.mul(out=oo, in_=ii, mul=sc[kc])
                    else:
                        ins = e.tensor_scalar(out=oo, in0=ii, scalar1=sc[kc], scalar2=None, op0=ALU.mult)
                return ins
            return f

        def conv_direct(p, slot):
            S.add("pool", lambda e: e.dma_start(out=wr[:, slot, :].rearrange("p (k c) -> p k c", k=16),
                                                in_=w_in_v[:, :, p * 512:(p + 1) * 512]),
                  writes=[("wr", slot)], dma_key=("wrc", slot))
            S.add("sp", lambda e: e.dma_start(out=win_s[p, :, :], in_=wr[:, slot, :]), reads=[("wr", slot)],
                  writes=[("win", p, "d")], dma_key=("wout_d", slot))
            conv_tok[("win", p)] = [("win", p, "d")]

        scratch_jobs = []
        for i in range(4):
            scratch_jobs.append((w_out_v[:, :, i * 512:(i + 1) * 512], wout_s[i, :, :], 16, ("wout", i)))
        for i in range(4):
            scratch_jobs.append((w_pe_v[:, :, i * 512:(i + 1) * 512], wpe_s[i, :, :], 2, ("wpe", i)))
            scratch_jobs.append((w_pg_v[:, :, i * 512:(i + 1) * 512], wpg_s[i, :, :], 16, ("wpg", i)))
        for (_, _, _, tok) in scratch_jobs:
            conv_tok[tok] = [tok + (0,)]
        for i in range(11):
            conv_tok[("win", i)] = [("win", i, "d")]

        def conv_scratch(n):
            for _ in range(n):
                if not scratch_jobs:
                    return
                src, dst, nk, tok = scratch_jobs.pop(0)
                S.add("pool", lambda e, src=src, dst=dst, nk=nk: e.dma_start(
                    out=dst.rearrange("p (k c) -> p k c", k=nk), in_=src),
                    writes=[tok + (0,)], dma_key=("cv",) + tok)

        first_use_extra = {"qr": [("qr", 0)], "sg1": [("sg", 0)], "tmpA1": [("tmpA", 0)]}

        tiles = []
        for (which, row0, nblk) in seqs:
            for b0 in range(0, nblk, NB):
                hl = 1 if b0 > 0 else 0
                hr = 1 if b0 + NB < nblk else 0
                tiles.append(dict(which=which, row0=row0, b0=b0, hl=hl, hr=hr, nkv=NB + hl + hr,
                                  prev_hl=(tiles[-1]["hl"] if hl else 0)))

        xh_ctr = [0]

        def stage0(T, as_list=False):
            out_list = []
            nkv, hl = T["nkv"], T["hl"]
            xsrc = xin[T["which"]]
            pos0 = (T["b0"] - hl) * 128
            def f_cs():
                S.add("sp", lambda e: e.dma_start(out=g_bc, in_=pre_g.partition_broadcast(128)),
                      writes=["gbc", ("Rr", 0), ("Rr", 1), ("tmpA", 0), ("tmpA", 1)], dma_key="gbc")
                S.add("sp", lambda e: e.dma_start(out=cs[:, 0:nkv, :],
                                                  in_=cs_d[pos0:pos0 + nkv * 128, :].rearrange("(s p) c -> p s c", p=128)),
                      writes=["cs"], dma_key="cs")
            out_list.append(f_cs)
            for s in range(hl, nkv):
                out_list.append(lambda s=s: s0_block(T, s, xsrc, pos0))
            if as_list:
                return out_list
            for f in out_list:
                pb_ = f()
                if pb_ is not None:
                    pb_()

        def s0_block(T, s, xsrc, pos0):
            if True:
                i = xh_ctr[0] % 2
                xh_ctr[0] += 1
                r0 = T["row0"] + pos0 + s * 128
                pr, pw = phase("XH", ("x", id(T)))
                S.add("sp", lambda e, i=i, r0=r0: e.dma_start(out=xh[:, i, :], in_=xsrc[r0:r0 + 128, :]),
                      reads=pr, writes=[("xh", i)] + pw + first_use_extra.pop("XH", []), dma_key=("xh", i))
                S.add("act", lambda e, i=i, s=s: e.activation(out=hb[:, i, :], in_=xh[:, i, :], func=AF.Square,
                                                              accum_out=col(C_SSQX + s)),
                      reads=[("xh", i), "XH"], writes=[("hb", i), ("st", C_SSQX + s)])
                S.add("pool", lambda e, s=s: e.tensor_scalar(out=col(C_RX + s), in0=col(C_SSQX + s), scalar1=1.0 / D,
                                                             scalar2=EPS, op0=ALU.mult, op1=ALU.add),
                      reads=[("st", C_SSQX + s)], writes=[("st", C_RX + s)])
                S.add("pool", lambda e, s=s: e.tensor_tensor(out=col(C_RX + s), in0=col(C_RX + s), in1=negh[:, 0:1], op=ALU.pow),
                      reads=[("st", C_RX + s), "negh"], writes=[("st", C_RX + s)])
                S.add("dve", lambda e, i=i, s=s: e.scalar_tensor_tensor(out=hb[:, i, :], in0=xh[:, i, :], scalar=col(C_RX + s),
                                                                        in1=g_bc, op0=ALU.mult, op1=ALU.mult),
                      reads=[("xh", i), ("st", C_RX + s), "XH", "gbc", ("Rr", 0), ("Rr", 1), ("tmpA", 0), ("tmpA", 1)],
                      writes=[("hb", i)])

            def partB(i=i, s=s):
                for half in range(2):
                    k = bank()

                    def f_tr(e, i=i, half=half, k=k):
                        for c in range(8):
                            ins = e.transpose(out=psb[k][:, c * 128:(c + 1) * 128],
                                              in_=hb[:, i, (half * 8 + c) * 128:(half * 8 + c + 1) * 128], identity=ident[:, :])
                        return ins
                    S.add("pe", f_tr, reads=[("hb", i), "ident"], writes=[("ps", k)])
                    pr, pw = phase("RB", ("h", id(T)))
                    eng = "act" if half == 0 else "dve"

                    def f_cp(e, k=k, half=half, s=s, eng=eng):
                        o = hT[:, half * 8:half * 8 + 8, s * 128:(s + 1) * 128]
                        i_ = psb[k][:, :].rearrange("p (c t) -> p c t", c=8)
                        return e.copy(out=o, in_=i_) if eng == "act" else e.tensor_copy(out=o, in_=i_)
                    S.add(eng, f_cp, reads=[("ps", k)] + pr, writes=[("hT", s)] + pw)
            return partB

        wr_ctr = [0]

        def load_piece(src, toks):
            s = wr_ctr[0] % 2
            wr_ctr[0] += 1
            S.add("sp", lambda e: e.dma_start(out=wr[:, s, :], in_=src), reads=toks, writes=[("wr", s)], dma_key=("wr", s))
            return s

        deferred = []

        def run_deferred():
            n = len(deferred)
            for _ in range(n):
                deferred.pop(0)()

        def mm_tok(T, ws, k, slot, ncols=512, c0=0):
            def f(e):
                for kc in range(16):
                    ins = e.matmul(psum[k][:, 0:ncols], lhsT=hT[:, kc, slot * 128:(slot + 1) * 128],
                                   rhs=wr[:, ws, kc * 512 + c0:kc * 512 + c0 + ncols], start=(kc == 0), stop=(kc == 15))
                return ins
            S.add("pe", f, reads=[("wr", ws), ("hT", slot), "RB"], writes=[("ps", k)])
            run_deferred()

        def mm_feat(T, ws, k, cg):
            hl = T["hl"]

            def f(e):
                for kc in range(16):
                    ins = e.matmul(psum[k][:, :], lhsT=wr[:, ws, kc * 512 + cg * 128:kc * 512 + (cg + 1) * 128],
                                   rhs=hT[:, kc, hl * 128:hl * 128 + 512], start=(kc == 0), stop=(kc == 15))
                return ins
            S.add("pe", f, reads=[("wr", ws), "RB"] + [("hT", hl + b) for b in range(NB)], writes=[("ps", k)])
            run_deferred()

        def rope_evac(k, nh, slot, dst_fn, dst_tok_r, dst_tok_w):
            w = nh * 128
            psv = psum[k][:, 0:w].rearrange("p (h t d) -> p h t d", h=nh, t=2)
            Av = ropeA[:, 0:w].rearrange("p (h t d) -> p h t d", h=nh, t=2)
            Bv = ropeB[:, 0:w].rearrange("p (h t d) -> p h t d", h=nh, t=2)
            qs = slot % 2
            qv = qr[:, qs, 0:w].rearrange("p (h t d) -> p h t d", h=nh, t=2)
            cosb = cs[:, slot, 0:64]
            sinb = cs[:, slot, 64:128]
            S.add("dve", lambda e: e.tensor_tensor(out=Av, in0=psv,
                                                   in1=cosb.unsqueeze(1).unsqueeze(1).to_broadcast([128, nh, 2, 64]), op=ALU.mult),
                  reads=[("ps", k), "cs"], writes=["ropeA"])
            S.add("dve", lambda e: e.tensor_tensor(out=Bv[:, :, 0, :], in0=psv[:, :, 1, :],
                                                   in1=sinb.unsqueeze(1).to_broadcast([128, nh, 64]), op=ALU.mult),
                  reads=[("ps", k), "cs"], writes=["ropeB0"])
            S.add("dve", lambda e: e.tensor_tensor(out=Bv[:, :, 1, :], in0=psv[:, :, 0, :],
                                                   in1=sinb.unsqueeze(1).to_broadcast([128, nh, 64]), op=ALU.mult),
                  reads=[("ps", k), "cs"], writes=["ropeB1"])
            S.add("pool", lambda e: e.tensor_tensor(out=qv[:, :, 0, :], in0=Av[:, :, 0, :], in1=Bv[:, :, 0, :], op=ALU.subtract),
                  reads=["ropeA", "ropeB0"], writes=[("qr", qs, 0)] + first_use_extra.pop("qr", []))
            S.add("pool", lambda e: e.tensor_tensor(out=qv[:, :, 1, :], in0=Av[:, :, 1, :], in1=Bv[:, :, 1, :], op=ALU.add),
                  reads=["ropeA", "ropeB1"], writes=[("qr", qs, 1)])
            def part2():
                k2 = bank()

                def f_tr(e):
                    for h in range(nh):
                        ins = e.transpose(out=psb[k2][:, h * 128:(h + 1) * 128], in_=qr[:, qs, h * 128:(h + 1) * 128],
                                          identity=ident[:, :])
                    return ins
                S.add("pe", f_tr, reads=[("qr", qs, 0), ("qr", qs, 1), ("qr", 0), "ident"], writes=[("ps", k2)])
                S.add("act", lambda e: e.copy(out=dst_fn(), in_=psb[k2][:, 0:w].rearrange("p (h t) -> p h t", h=nh)),
                      reads=[("ps", k2)] + dst_tok_r, writes=dst_tok_w)
            deferred.append(part2)

        def piece_kv(T, ws):
            if T["hl"]:
                ps_ = T["prev_hl"] + NB - 1
                for d_, s_ in ((0, ps_), (1, ps_ + 1)):
                    S.add("pool", lambda e, d_=d_, s_=s_: e.tensor_copy(out=kT[:, :, d_ * 128:(d_ + 1) * 128],
                                                                        in_=kT[:, :, s_ * 128:(s_ + 1) * 128]),
                          reads=[("kT", s_)], writes=[("kT", d_)])
                    S.add("pool", lambda e, d_=d_, s_=s_: e.tensor_copy(out=vv[:, d_, :], in_=vv[:, s_, :]),
                          reads=[("vv", s_)], writes=[("vv", d_)])
            for s in range(2 * T["hl"], T["nkv"]):
                k = bank()
                mm_tok(T, ws, k, s)
                rope_evac(k, 2, s, lambda s=s: kT[:, :, s * 128:(s + 1) * 128], [], [("kT", s)])
                S.add("dve", lambda e, k=k, s=s: e.tensor_copy(out=vv[:, s, :], in_=psum[k][:, 256:512]),
                      reads=[("ps", k)], writes=[("vv", s)])

        def piece_q(T, ws, g):
            for b in range(NB):
                k = bank()
                mm_tok(T, ws, k, T["hl"] + b)
                pr, pw = phase("QT", ("q", id(T)))
                rope_evac(k, 4, T["hl"] + b, lambda b=b: qT[:, 4 * g:4 * g + 4, b * 128:(b + 1) * 128], pr, [("qT", g, b)] + pw)

        def piece_gate(T, ws, which, half, hooks=None):
            for cg in range(4):
                if hooks and cg > 0 and hooks[cg - 1] is not None:
                    hooks[cg - 1]()
                k = bank()
                mm_feat(T, ws, k, cg)
                h = 4 * half + cg
                if which == "ga":
                    pr, pw = phase("GU", ("g", id(T)))
                    ex = first_use_extra.pop("GU", [])
                    S.add("act", lambda e, k=k, h=h: e.activation(out=gaT[:, h, :], in_=psum[k][:, :], func=AF.Silu),
                          reads=[("ps", k)] + pr, writes=[("gaT", h)] + pw + ex)
                elif which == "u":
                    pr, pw = phase("GU", ("g", id(T)))
                    S.add("act", lambda e, k=k, h=h: e.activation(out=uT[:, h, :], in_=psum[k][:, :], func=AF.Gelu),
                          reads=[("ps", k)] + pr, writes=[("uT", h)] + pw)
                else:
                    s2 = cg % 2
                    S.add("act", lambda e, k=k, s2=s2: e.activation(out=sg[:, s2, :], in_=psum[k][:, :], func=AF.Silu),
                          reads=[("ps", k)], writes=[("sg", s2)] + (first_use_extra.pop("sg1", []) if s2 == 1 else []))
                    S.add("dve", lambda e, h=h, s2=s2: e.tensor_tensor(out=uT[:, h, :], in0=uT[:, h, :], in1=sg[:, s2, :],
                                                                       op=ALU.mult),
                          reads=[("sg", s2), ("uT", h), "GU"], writes=[("uT", h)])
            if hooks and hooks[3] is not None:
                hooks[3]()

        def piece_vg(T, ws, half):
            for b in range(NB):
                k = bank()
                mm_tok(T, ws, k, T["hl"] + b)
                pr, pw = phase("XH", ("gv", id(T)))
                S.add("act", lambda e, k=k, b=b: e.activation(out=gv[:, b, half * 512:(half + 1) * 512], in_=psum[k][:, :],
                                                              func=AF.Gelu, accum_out=col(C_GSUM + 2 * b + half)),
                      reads=[("ps", k)] + pr, writes=[("gv", b, half), ("st", C_GSUM + 2 * b + half)] + pw)
                if half == 1:
                    S.add("act", lambda e, b=b: e.activation(out=junk[:, :], in_=gv[:, b, :], func=AF.Square,
                                                             accum_out=col(C_GSSQ + b)),
                          reads=[("gv", b, 0), ("gv", b, 1), "XH"], writes=[("st", C_GSSQ + b), ("junk", 0), ("junk", 1)])
            if half == 1:
                gs = stt[:, C_GSUM:C_GSUM + 8].rearrange("p (b t) -> p b t", t=2)
                mean = stt[:, C_GMEAN:C_GMEAN + 4]
                msq = stt[:, C_GMSQ:C_GMSQ + 4]
                gr = stt[:, C_GR:C_GR + 4]
                gssq = stt[:, C_GSSQ:C_GSSQ + 4]
                rs = [("st", C_GSUM + i) for i in range(8)]
                S.add("dve", lambda e: e.tensor_tensor(out=mean, in0=gs[:, :, 0], in1=gs[:, :, 1], op=ALU.add),
                      reads=rs, writes=["gmean"])
                S.add("dve", lambda e: e.tensor_scalar(out=mean, in0=mean, scalar1=1.0 / 1024, scalar2=None, op0=ALU.mult),
                      reads=["gmean"], writes=["gmean"])
                S.add("dve", lambda e: e.tensor_tensor(out=msq, in0=mean, in1=mean, op=ALU.mult), reads=["gmean"], writes=["gmsq"])
                S.add("dve", lambda e: e.scalar_tensor_tensor(out=gr, in0=gssq, scalar=1.0 / 1024, in1=msq, op0=ALU.mult,
                                                              op1=ALU.subtract),
                      reads=["gmsq"] + [("st", C_GSSQ + b) for b in range(4)], writes=["gr"])
                S.add("dve", lambda e: e.tensor_scalar(out=gr, in0=gr, scalar1=EPS, scalar2=None, op0=ALU.add),
                      reads=["gr"], writes=["gr"])
                S.add("pool", lambda e: e.tensor_tensor(out=gr, in0=gr, in1=negh[:, 0:4], op=ALU.pow),
                      reads=["gr", "negh"], writes=["gr"])
                for b in range(NB):
                    pr, pw = phase("NN", ("n", id(T)))
                    S.add("dve", lambda e, b=b: e.tensor_scalar(out=nn[:, b, :], in0=gv[:, b, :],
                                                                 scalar1=stt[:, C_GMEAN + b:C_GMEAN + b + 1],
                                                                 scalar2=stt[:, C_GR + b:C_GR + b + 1],
                                                                 op0=ALU.subtract, op1=ALU.mult),
                          reads=[("gv", b, 0), ("gv", b, 1), "gmean", "gr", "XH"] + pr,
                          writes=[("nn", b)] + pw + first_use_extra.pop("NN", []))

        def att_units(T):
            nblk_seq = None
            return [(n, g) for n in range(NB) for g in range(2)]

        att_state = {}

        def att_qk(T, u, ui):
            n, g = u
            hl, nkv = T["hl"], T["nkv"]
            own = hl + n
            js = [s for s in (own - 1, own, own + 1) if 0 <= s < nkv]
            ps_ = ui % 2
            banks = []
            for jj, s in enumerate(js):
                k = bank()
                banks.append(k)
                S.add("pe", lambda e, k=k, s=s: e.matmul(psum[k][:, :].rearrange("p (h q) -> p h q", h=4), lhsT=kT[:, g, s * 128:(s + 1) * 128],
                                                         rhs=qT[:, 4 * g:4 * g + 4, n * 128:(n + 1) * 128], start=True, stop=True),
                      reads=[("kT", s), ("qT", g, n), "QT"], writes=[("ps", k)])
                S.add("act", lambda e, k=k, jj=jj: e.activation(out=PT[:, ps_, jj, :], in_=psum[k][:, :], func=AF.Exp, scale=SCALE),
                      reads=[("ps", k)], writes=[("PT", ps_, jj)])
                if s != own:
                    m = maskP if s < own else maskN
                    S.add("pool", lambda e, jj=jj, m=m: e.tensor_tensor(
                        out=PT[:, ps_, jj, :].rearrange("p (h q) -> p h q", h=4),
                        in0=PT[:, ps_, jj, :].rearrange("p (h q) -> p h q", h=4),
                        in1=m[:, :].unsqueeze(1).to_broadcast([128, 4, 128]), op=ALU.mult),
                        reads=[("PT", ps_, jj), "maskP", "maskN"], writes=[("PT", ps_, jj)])
            att_state[ui] = js

        def att_pv(T, u, ui):
            n, g = u
            js = att_state.pop(ui)
            ps_ = ui % 2
            kO, kD = bank(), bank()

            def f_pv(e):
                for jj, s in enumerate(js):
                    ins = e.matmul(psum[kO][:, :], lhsT=vv[:, s, g * 128:(g + 1) * 128], rhs=PT[:, ps_, jj, :],
                                   start=(jj == 0), stop=(jj == len(js) - 1))
                return ins

            def f_d(e):
                for jj, s in enumerate(js):
                    ins = e.matmul(psum[kD][:, :], lhsT=ones[:, :], rhs=PT[:, ps_, jj, :],
                                   start=(jj == 0), stop=(jj == len(js) - 1))
                return ins
            ptr = [("PT", ps_, jj) for jj in range(len(js))]
            S.add("pe", f_pv, reads=ptr + [("vv", s) for s in js], writes=[("ps", kO)])
            S.add("pe", f_d, reads=ptr + ["ones"], writes=[("ps", kD)])
            rs = ui % 2

            def f_r(e):
                for h in range(4):
                    ins = e.tensor_scalar(out=Rr[:, rs, h * 128:(h + 1) * 128], in0=psum[kD][:, h * 128:(h + 1) * 128],
                                          scalar1=esink[:, 4 * g + h:4 * g + h + 1], scalar2=None, op0=ALU.add)
                return ins
            S.add("dve", f_r, reads=[("ps", kD), "esink"], writes=[("Rr", rs)])
            S.add("dve", lambda e: e.reciprocal(out=Rr[:, rs, :], in_=Rr[:, rs, :]), reads=[("Rr", rs)], writes=[("Rr", rs)])
            S.add("dve", lambda e: e.tensor_tensor(out=tmpA[:, rs, :], in0=psum[kO][:, :], in1=Rr[:, rs, :], op=ALU.mult),
                  reads=[("ps", kO), ("Rr", rs)], writes=[("tmpA", rs)] + (first_use_extra.pop("tmpA1", []) if rs == 1 else []))
            pr, pw = phase("XH", ("am", id(T)))
            S.add("dve", lambda e: e.tensor_tensor(out=amT[:, 4 * g:4 * g + 4, n * 128:(n + 1) * 128],
                                                    in0=tmpA[:, rs, :].rearrange("p (h q) -> p h q", h=4),
                                                    in1=gaT[:, 4 * g:4 * g + 4, n * 128:(n + 1) * 128], op=ALU.mult),
                  reads=[("tmpA", rs), "GU"] + [("gaT", 4 * g + h) for h in range(4)] + pr,
                  writes=[("amT", 0, n, g)] + pw)

        sp_ctr = [0]

        def spatial(T, b, hq):
            k = bank()
            rs = sp_ctr[0] % 2
            sp_ctr[0] += 1

            def f(e):
                for h in range(4):
                    hh = 4 * hq + h
                    ins = e.matmul(psum[k][:, h * 128:(h + 1) * 128], lhsT=nn[:, b, hh * 128:(hh + 1) * 128],
                                   rhs=wsT[:, hh, :], start=True, stop=True)
                return ins
            S.add("pe", f, reads=[("nn", b), "wsT", "NN"], writes=[("ps", k)])

            def f2(e):
                for h in range(4):
                    hh = 4 * hq + h
                    ins = e.scalar_tensor_tensor(out=tmpA[:, rs, h * 128:(h + 1) * 128], in0=psum[k][:, h * 128:(h + 1) * 128],
                                                 scalar=lng[:, hh:hh + 1], in1=Bias[:, hh, :], op0=ALU.mult, op1=ALU.add)
                return ins
            S.add("dve", f2, reads=[("ps", k), "lng"] + [("Bias", 4 * hq + h) for h in range(4)],
                  writes=[("tmpA", rs)] + (first_use_extra.pop("tmpA1", []) if rs == 1 else []))
            pr, pw = phase("XH", ("am", id(T)))
            S.add("dve", lambda e: e.tensor_tensor(out=amT[:, 8 + 4 * hq:8 + 4 * hq + 4, b * 128:(b + 1) * 128],
                                                    in0=tmpA[:, rs, :].rearrange("p (h q) -> p h q", h=4),
                                                    in1=uT[:, 4 * hq:4 * hq + 4, b * 128:(b + 1) * 128], op=ALU.mult),
                  reads=[("tmpA", rs), "GU"] + [("uT", 4 * hq + h) for h in range(4)] + pr,
                  writes=[("amT", 1, b, hq)] + pw)

        def piece_wout(T, ws, i):
            if i == 0:
                x_reload(T)
            if i == 0:
                p_prep(T)
            if i == 3:
                p_prep_b(T)
            if i == 2:
                T["_wps"][0] = load_wpe(0)
            for b in range(NB):
                k = bank()

                def f(e, k=k, b=b):
                    for fc in range(16):
                        ins = e.matmul(psum[k][:, :], lhsT=amT[:, fc, b * 128:(b + 1) * 128],
                                       rhs=wr[:, ws, fc * 512:(fc + 1) * 512], start=(fc == 0), stop=(fc == 15))
                    return ins
                S.add("pe", f, reads=[("wr", ws), "XH"] + [("amT", 0, b, g) for g in range(2)] + [("amT", 1, b, g) for g in range(2)],
                      writes=[("ps", k)])
                ex = first_use_extra.pop("ys", [])
                S.add("act", lambda e, k=k, b=b: e.copy(out=ys[:, b, i * 512:(i + 1) * 512], in_=psum[k][:, :]),
                      reads=[("ps", k)], writes=[("ys", b, i)] + ex)
                js = b % 2
                S.add("act", lambda e, k=k, b=b, js=js: e.activation(out=junk[:, js * 512:(js + 1) * 512], in_=psum[k][:, :],
                                                                     func=AF.Square, accum_out=col(C_YSSQ + 4 * b + i)),
                      reads=[("ps", k)], writes=[("st", C_YSSQ + 4 * b + i), ("junk", js)])
                if i == 3:
                    if b >= 2:
                        x1_block_b(T, b - 2)
                    x1_block_a(T, b)
            if i == 3:
                T["_x1last"] = [lambda: x1_block_b(T, NB - 2), lambda: x1_block_b(T, NB - 1)]

        def x_reload(T):
            xsrc = xin[T["which"]]
            for b in range(NB):
                r0 = T["row0"] + (T["b0"] + b) * 128
                if b < 3:
                    pr, pw = phase("RB", ("xr", id(T)))
                    S.add("sp", lambda e, b=b, r0=r0: e.dma_start(out=xrl[b], in_=xsrc[r0:r0 + 128, :]), reads=pr,
                          writes=[("xrl", b)] + pw, dma_key=("xrl", b))
                else:
                    S.add("sp", lambda e, b=b, r0=r0: e.dma_start(out=xrl[b], in_=xsrc[r0:r0 + 128, :]),
                          writes=[("xrl", b), ("hb", 0), ("hb", 1)], dma_key=("xrl", b))

        def x1_block_a(T, b):
            ysb = [("ys", b, i_) for i_ in range(4)]
            yt, yr = col(C_YT + b), col(C_YR + b)
            c_ = [col(C_YSSQ + 4 * b + j) for j in range(4)]
            S.add("pool", lambda e: e.tensor_scalar(out=yt, in0=c_[0], scalar1=c_[1], scalar2=c_[2], op0=ALU.add, op1=ALU.add),
                  reads=[("st", C_YSSQ + 4 * b + j) for j in range(4)], writes=[("st", C_YT + b)])
            S.add("pool", lambda e: e.tensor_scalar(out=yt, in0=yt, scalar1=c_[3], scalar2=None, op0=ALU.add),
                  reads=[("st", C_YSSQ + 4 * b + 3), ("st", C_YT + b)], writes=[("st", C_YT + b)])
            S.add("pool", lambda e: e.tensor_scalar(out=yr, in0=yt, scalar1=1.0 / D, scalar2=EPS, op0=ALU.mult, op1=ALU.add),
                  reads=[("st", C_YT + b)], writes=[("st", C_YR + b)])
            S.add("pool", lambda e: e.tensor_tensor(out=yr, in0=yr, in1=negh[:, 0:1], op=ALU.pow),
                  reads=[("st", C_YR + b), "negh"], writes=[("st", C_YR + b)])
            S.add("dve", lambda e: e.scalar_tensor_tensor(out=ys[:, b, :], in0=ys[:, b, :], scalar=yr, in1=postg[:, :],
                                                          op0=ALU.mult, op1=ALU.mult),
                  reads=ysb + [("st", C_YR + b), "postg"], writes=ysb)
            S.add("dve", lambda e: e.tensor_tensor(out=ys[:, b, :], in0=ys[:, b, :], in1=xrl[b], op=ALU.add),
                  reads=ysb + [("xrl", b)] + (["RB"] if b < 3 else [("hb", 0), ("hb", 1)]), writes=ysb)
            S.add("act", lambda e: e.copy(out=x1b[:, b % 2, :], in_=ys[:, b, :]), reads=ysb, writes=[("x1b", b % 2)])

        def x1_block_b(T, b):
            for half in range(2):
                k = bank()

                def f_tr(e, half=half, k=k, b=b):
                    for c in range(8):
                        ins = e.transpose(out=psb[k][:, c * 128:(c + 1) * 128],
                                          in_=x1b[:, b % 2, (half * 8 + c) * 128:(half * 8 + c + 1) * 128],
                                          identity=ident[:, :])
                    return ins
                S.add("pe", f_tr, reads=[("x1b", b % 2), "ident"], writes=[("ps", k)])
                pr, pw = phase("GU", ("x", id(T)))

                eng = "act" if half == 0 else "dve"

                def f_cp(e, k=k, half=half, b=b, eng=eng):
                    o = x1T[:, half * 8:half * 8 + 8, b * 128:(b + 1) * 128]
                    i_ = psb[k][:, :].rearrange("p (c t) -> p c t", c=8)
                    return e.copy(out=o, in_=i_) if eng == "act" else e.tensor_copy(out=o, in_=i_)
                S.add(eng, f_cp, reads=[("ps", k)] + pr, writes=[("x1T", b, half)] + pw)

        def p_prep(T):
            psrc = pin[T["which"]]
            r0 = T["row0"] + T["b0"] * 128
            pr, pw = phase("QT", ("p", id(T)))
            S.add("sp", lambda e: e.dma_start(out=pf, in_=psrc[r0:r0 + NB * 128, :].rearrange("(b p) c -> p b c", p=128)),
                  reads=pr, writes=["pf"] + pw, dma_key="pf")
            S.add("dve", lambda e: e.tensor_copy(out=pbb, in_=pf), reads=["pf", "QT"], writes=["pbb"])

        def p_prep_b(T):
            k = bank()

            def f_tr(e):
                for b in range(NB):
                    for c in range(2):
                        ins = e.transpose(out=psb[k][:, (c * 4 + b) * 128:(c * 4 + b + 1) * 128],
                                          in_=pbb[:, b, c * 128:(c + 1) * 128], identity=ident[:, :])
                return ins
            S.add("pe", f_tr, reads=["pbb", "ident", "QT"], writes=[("ps", k)])
            S.add("act", lambda e: e.copy(out=pT.rearrange("p c t -> p (c t)"), in_=psb[k][:, :]),
                  reads=[("ps", k), "QT"], writes=["pT"])

        wpe_ctr = [0]

        def load_wpe(i):
            s = wpe_ctr[0] % 2
            wpe_ctr[0] += 1
            S.add("sp", lambda e: e.dma_start(out=wpe[:, s, :], in_=wpe_s[i, :, :]), reads=conv_tok[("wpe", i)],
                  writes=[("wpe", s)], dma_key=("wpe", s))
            return s

        s5_ctr = [0]

        def piece_pg(T, ws, wps, i):
            yd = yout[T["which"]]
            for b in range(NB):
                kA, kB = bank(), bank()

                def fA(e, kA=kA, b=b):
                    for kc in range(16):
                        ins = e.matmul(psum[kA][:, :], lhsT=x1T[:, kc, b * 128:(b + 1) * 128],
                                       rhs=wr[:, ws, kc * 512:(kc + 1) * 512], start=(kc == 0), stop=(kc == 15))
                    return ins

                def fB(e, kB=kB, b=b):
                    for c in range(2):
                        ins = e.matmul(psum[kB][:, :], lhsT=pT[:, c, b * 128:(b + 1) * 128],
                                       rhs=wpe[:, wps, c * 512:(c + 1) * 512], start=(c == 0), stop=(c == 1))
                    return ins
                S.add("pe", fA, reads=[("wr", ws), ("x1T", b, 0), ("x1T", b, 1), "GU"], writes=[("ps", kA)])
                S.add("pe", fB, reads=[("wpe", wps), "pT", "QT"], writes=[("ps", kB)])
                s = s5_ctr[0] % 2
                s5_ctr[0] += 1
                pr, pw = phase("NN", ("s", id(T)))
                S.add("act", lambda e, kA=kA, s=s: e.activation(out=sig[:, s, :], in_=psum[kA][:, :], func=AF.Sigmoid),
                      reads=[("ps", kA)] + pr, writes=[("sig", s)] + pw)
                S.add("dve", lambda e, kB=kB, s=s: e.tensor_tensor(out=tmp2[:, s, :], in0=psum[kB][:, :], in1=sig[:, s, :],
                                                                  op=ALU.mult),
                      reads=[("ps", kB), ("sig", s), "NN"], writes=[("tmp2", s)])
                S.add("dve", lambda e, b=b, s=s: e.tensor_tensor(out=ys[:, b, i * 512:(i + 1) * 512],
                                                                 in0=ys[:, b, i * 512:(i + 1) * 512], in1=tmp2[:, s, :], op=ALU.add),
                      reads=[("tmp2", s), ("ys", b, i), "NN"], writes=[("ys", b, i)])
                gidx = i * NB + b
                if gidx <= 1 and T.get("_x1last"):
                    T["_x1last"].pop(0)()
                pend = T.setdefault("_s0pend", [])
                if gidx % 2 == 0:
                    if len(pend) >= 2:
                        pend.pop(0)()
                    if T.get("_s0next"):
                        if gidx == 0:
                            T["_s0next"].pop(0)()
                        if T["_s0next"]:
                            pend.append(T["_s0next"].pop(0)())
                if i == 3 and b == NB - 1:
                    while pend:
                        pend.pop(0)()
                    while T.get("_s0next"):
                        T["_s0next"].pop(0)()()
                if i == 3:
                    r0 = T["row0"] + (T["b0"] + b) * 128
                    S.add("sp", lambda e, b=b, r0=r0: e.dma_start(out=yd[r0:r0 + 128, :], in_=ys[:, b, :]),
                          reads=[("ys", b, i_) for i_ in range(4)], dma_key=("o", b))

        plan = []
        for ti, T in enumerate(tiles):
            T["_wps"] = {}

            def add_piece(src, toks, fn):
                plan.append((src, toks, fn))

            units = [(n, g) for g in range(2) for n in range(NB)]
            add_piece(win_s[2], conv_tok[("win", 2)], lambda ws, T=T: piece_kv(T, ws))
            add_piece(win_s[0], conv_tok[("win", 0)], lambda ws, T=T: piece_q(T, ws, 0))
            add_piece(win_s[1], conv_tok[("win", 1)], lambda ws, T=T: piece_q(T, ws, 1))
            add_piece(win_s[7], conv_tok[("win", 7)], lambda ws, T=T: piece_vg(T, ws, 0))
            add_piece(win_s[8], conv_tok[("win", 8)], lambda ws, T=T: piece_vg(T, ws, 1))

            def hk(pv_list, qk_list, T=T, units=units):
                h = [None, None, None, None]
                for j, ui in enumerate(pv_list):
                    h[2 * j] = (lambda ui=ui: att_pv(T, units[ui], ui))
                for j, ui in enumerate(qk_list):
                    h[2 * j + 1] = (lambda ui=ui: att_qk(T, units[ui], ui))
                return h
            add_piece(win_s[3], conv_tok[("win", 3)], lambda ws, T=T, hk=hk: piece_gate(T, ws, "ga", 0, hk([], [0, 1])))
            add_piece(win_s[4], conv_tok[("win", 4)], lambda ws, T=T, hk=hk: piece_gate(T, ws, "ga", 1, hk([0, 1], [2, 3])))
            add_piece(win_s[5], conv_tok[("win", 5)], lambda ws, T=T, hk=hk: piece_gate(T, ws, "u", 0, hk([2, 3], [4, 5])))
            add_piece(win_s[6], conv_tok[("win", 6)], lambda ws, T=T, hk=hk: piece_gate(T, ws, "u", 1, hk([4, 5], [6, 7])))
            add_piece(win_s[9], conv_tok[("win", 9)], lambda ws, T=T, hk=hk: piece_gate(T, ws, "gg", 0, hk([6, 7], [])))
            add_piece(win_s[10], conv_tok[("win", 10)], lambda ws, T=T: piece_gate(T, ws, "gg", 1))

            def p_sp(ws, T=T, units=units):
                for b in range(4):
                    for hq in range(2):
                        spatial(T, b, hq)
            add_piece(None, [], p_sp)
            for i in range(4):
                add_piece(wout_s[i], conv_tok[("wout", i)], lambda ws, T=T, i=i: piece_wout(T, ws, i))

            def p_x1(ws, T=T, ti=ti):

                T["_s0next"] = stage0(tiles[ti + 1], as_list=True) if ti + 1 < len(tiles) else []
            add_piece(None, [], p_x1)

            def p_pg(ws, T=T, i=0):
                if i + 1 < 4:
                    T["_wps"][i + 1] = load_wpe(i + 1)
                piece_pg(T, ws, T["_wps"][i], i)
            for i in range(4):
                add_piece(wpg_s[i], conv_tok[("wpg", i)], lambda ws, T=T, i=i, p_pg=p_pg: p_pg(ws, T, i))

        stage0(tiles[0])
        real = [pi for pi, p in enumerate(plan) if p[0] is not None]
        nxt_real = {}
        for a, b_ in zip(real[:-1], real[1:]):
            nxt_real[a] = b_
        loaded = {}

        n_t0 = len(plan) // len(tiles)
        win_order = [2, 0, 1, 7, 8, 3, 4, 5, 6, 9, 10]

        def ensure_loaded(pi):
            if pi not in loaded:
                if pi < 11:
                    slot = wr_ctr[0] % 2
                    wr_ctr[0] += 1
                    conv_direct(win_order[pi], slot)
                    loaded[pi] = slot
                else:
                    loaded[pi] = load_piece(plan[pi][0], plan[pi][1])

        mark("stage0_done")
        ensure_loaded(real[0])
        for pi, (src, toks, fn) in enumerate(plan):
            mark("plan%d" % pi)
            if src is not None:
                ensure_loaded(pi)
                if pi in nxt_real and nxt_real[pi] < 11:
                    ensure_loaded(nxt_real[pi])
                if pi < 11:
                    conv_scratch(2 if pi < 2 else 1)
                if pi in nxt_real:
                    ensure_loaded(nxt_real[pi])
                fn(loaded[pi])
            else:
                fn(None)

        if max_ops is not None:
            S.ops = S.ops[:max_ops]
        S.emit(nc, st)
    return nc


def _cs_table():
    inv = 1.0 / (10000.0 ** (np.arange(0, 128, 2, dtype=np.float32) / np.float32(128)))
    ang = np.arange(4096, dtype=np.float32)[:, None] * inv[None, :].astype(np.float32)
    return np.concatenate([np.cos(ang), np.sin(ang)], axis=1).astype(np.float32)


_NC_CACHE = {}


def _weights_map(pre_norm_g, w_in, attn_sink, gmlp_ln_g, gmlp_ln_b, gmlp_ws, gmlp_bs, w_out, post_norm_g, w_pe, w_pg):
    c = np.ascontiguousarray
    return {
        "w_in": c(w_in[0]), "w_out": c(w_out[0]), "w_pg": c(w_pg[0]), "w_pe": c(w_pe[0]),
        "pre_g": c(pre_norm_g[0:1]), "post_g": c(post_norm_g[0:1]), "sink": c(attn_sink[0:1]),
        "ln_g": c(gmlp_ln_g[0:1]), "ln_b": c(gmlp_ln_b[0:1]), "ws": c(gmlp_ws[0]),
        "bs": c(gmlp_bs[0].reshape(1, 1024)), "cs": _cs_table(),
    }


def kernel(x_prompt, x_sample, p_prompt, p_sample, pre_norm_g, w_in, attn_sink, gmlp_ln_g, gmlp_ln_b,
           gmlp_ws, gmlp_bs, w_out, post_norm_g, w_pe, w_pg):
    f = lambda a: np.asarray(a, dtype=np.float32)
    x_prompt, x_sample, p_prompt, p_sample = f(x_prompt), f(x_sample), f(p_prompt), f(p_sample)
    wm = _weights_map(f(pre_norm_g), f(w_in), f(attn_sink), f(gmlp_ln_g), f(gmlp_ln_b), f(gmlp_ws), f(gmlp_bs),
                      f(w_out), f(post_norm_g), f(w_pe), f(w_pg))
    n = 8
    seqs = [("a", 0, 32), ("b", 0, 16), ("b", 2048, 16)]
    key = "full"
    if key not in _NC_CACHE:
        _NC_CACHE[key] = build(seqs, 4096, 4096)
    nc = _NC_CACHE[key]
    in_maps = []
    for c in range(n):
        m = dict(wm)
        m["xa"] = np.ascontiguousarray(x_prompt[c])
        m["xb"] = np.ascontiguousarray(x_sample[2 * c:2 * c + 2].reshape(4096, D))
        m["pa"] = np.ascontiguousarray(p_prompt[0, c])
        m["pb"] = np.ascontiguousarray(p_sample[0, 2 * c:2 * c + 2].reshape(4096, 256))
        in_maps.append(m)
    res = run_bass_kernel_spmd(nc, in_maps, core_ids=list(range(n)))
    y_prompt = np.stack([np.asarray(r["ya"], dtype=np.float32) for r in res.results], axis=0)
    y_sample = np.concatenate([np.asarray(r["yb"], dtype=np.float32).reshape(2, 2048, D) for r in res.results], axis=0)
    return (y_prompt, y_sample)
```

```python
import math
from contextlib import ExitStack

import numpy as np
import concourse.bass as bass
import concourse.mybir as mybir
from concourse.bass_utils import run_bass_kernel_spmd

F32 = mybir.dt.float32
BF16 = mybir.dt.bfloat16
AF = mybir.ActivationFunctionType
ALU = mybir.AluOpType
AX = mybir.AxisListType

D = 2048
INW = 5632
NB = 4
EPS = 1e-6
SCALE = 128.0 ** -0.5


class _Op:
    __slots__ = ("eng", "fn", "deps", "dma", "key", "sig", "cnt")

    def __init__(self, eng, fn, deps, key):
        self.eng = eng
        self.fn = fn
        self.deps = deps
        self.dma = key is not None
        self.key = key
        self.sig = False
        self.cnt = 0


class Sched:
    ENGS = ("pe", "act", "dve", "pool", "sp")

    def __init__(self):
        self.ops = []
        self.lastw = {}
        self.readers = {}
        self.group_keys = set()

    def add(self, eng, fn, reads=(), writes=(), dma_key=None):
        idx = len(self.ops)
        deps = set()
        for r in reads:
            w = self.lastw.get(r)
            if w is not None:
                deps.add(w)
        for w_ in writes:
            w = self.lastw.get(w_)
            if w is not None:
                deps.add(w)
            rl = self.readers.get(w_)
            if rl:
                deps.update(rl)
        for r in reads:
            self.readers.setdefault(r, []).append(idx)
        for w_ in writes:
            self.lastw[w_] = idx
            self.readers[w_] = []
        deps.discard(idx)
        self.ops.append(_Op(eng, fn, deps, dma_key))
        return idx

    def emit(self, nc, stack):
        ops = self.ops
        for op in ops:
            nd = set()
            for d in op.deps:
                p = ops[d]
                if (not p.dma) and p.eng == op.eng and p.eng in ("pe", "sp"):
                    continue
                if p.dma and op.dma and p.key == op.key and p.key in self.group_keys:
                    continue
                nd.add(d)
            op.deps = nd
            for d in nd:
                ops[d].sig = True
        esem = {e: stack.enter_context(nc.semaphore("s_" + e)) for e in self.ENGS}
        dsem, dcount = {}, {}
        ecount = {e: 0 for e in self.ENGS}
        for op in ops:
            if op.dma:
                op.sig = True
                if op.key not in dsem:
                    dsem[op.key] = stack.enter_context(nc.semaphore("d%d" % len(dsem)))
                    dcount[op.key] = 0
                dcount[op.key] += 16
                op.cnt = dcount[op.key]
            elif op.sig:
                ecount[op.eng] += 1
                op.cnt = ecount[op.eng]
        per_eng = {e: [] for e in self.ENGS}
        for op in ops:
            per_eng[op.eng].append(op)
        block = stack.enter_context(nc.Block())
        group_keys = self.group_keys

        def run(e, engine):
            waited = {}
            for op in per_eng[e]:
                need = {}
                for d in op.deps:
                    p = ops[d]
                    if p.dma:
                        s = dsem[p.key]
                        v = dcount[p.key] if p.key in group_keys else p.cnt
                    else:
                        s = esem[p.eng]
                        v = p.cnt
                    k = id(s)
                    if k not in need or need[k][1] < v:
                        need[k] = (s, v)
                for k, (s, v) in need.items():
                    if waited.get(k, 0) >= v:
                        continue
                    waited[k] = v
                    engine.wait_ge(s, v)
                ins = op.fn(engine)
                if op.sig:
                    if op.dma:
                        ins.then_inc(dsem[op.key], 16)
                    else:
                        ins.then_inc(esem[op.eng], 1)
            if e == "sp":
                for k, s in dsem.items():
                    engine.wait_ge(s, dcount[k])

        @block.tensor
        def _(eng):
            run("pe", eng)

        @block.scalar
        def _(eng):
            run("act", eng)

        @block.vector
        def _(eng):
            run("dve", eng)

        @block.gpsimd
        def _(eng):
            run("pool", eng)

        @block.sync
        def _(eng):
            run("sp", eng)


def build(seqs, rows_a, rows_b, max_ops=None, marks=None):
    nc = bass.Bass("TRN2", target_bir_lowering=False)

    def din(name, shape):
        return nc.dram_tensor(name, shape, F32, kind="ExternalInput").ap()

    xin = {"a": din("xa", [rows_a, D]), "b": din("xb", [rows_b, D])}
    pin = {"a": din("pa", [rows_a, 256]), "b": din("pb", [rows_b, 256])}
    yout = {"a": nc.dram_tensor("ya", [rows_a, D], F32, kind="ExternalOutput").ap(),
            "b": nc.dram_tensor("yb", [rows_b, D], F32, kind="ExternalOutput").ap()}
    w_in = din("w_in", [D, INW])
    w_out = din("w_out", [D, D])
    w_pg = din("w_pg", [D, D])
    w_pe = din("w_pe", [256, D])
    pre_g = din("pre_g", [1, D])
    post_g = din("post_g", [1, D])
    sink = din("sink", [1, 8])
    ln_g = din("ln_g", [1, 1024])
    ln_b = din("ln_b", [1, 1024])
    ws_d = din("ws", [8, 128, 128])
    bs_d = din("bs", [1, 1024])
    cs_d = din("cs", [4096, 128])
    win_s = nc.dram_tensor("win_s", [11, 128, 8192], BF16).ap()
    wout_s = nc.dram_tensor("wout_s", [4, 128, 8192], BF16).ap()
    wpg_s = nc.dram_tensor("wpg_s", [4, 128, 8192], BF16).ap()
    wpe_s = nc.dram_tensor("wpe_s", [4, 128, 1024], BF16).ap()

    S = Sched()
    S.group_keys.add("setup")
    with ExitStack() as st:
        st.enter_context(nc.allow_non_contiguous_dma(reason="tiny per-partition parameter loads"))

        def sb(name, shape, dt):
            return st.enter_context(nc.sbuf_tensor("sb_" + name, shape, dt))

        RBt = sb("RB", [128, 12288], BF16)
        hT = RBt[:, :].rearrange("p (k t) -> p k t", k=16)
        hb = sb("hb", [128, 2, 2048], BF16)
        _rbf = RBt[:, :].bitcast(F32)
        xrl = [_rbf[:, 0:2048], _rbf[:, 2048:4096], _rbf[:, 4096:6144],
               hb[:, :, :].rearrange("p s c -> p (s c)").bitcast(F32)]
        x1b = sb("x1b", [128, 2, 2048], BF16)
        XHt = sb("XH", [128, 4096], F32)
        xh = XHt[:, :].rearrange("p (s c) -> p s c", s=2)
        gv = XHt[:, :].rearrange("p (b c) -> p b c", b=4)
        amT = XHt[:, :].bitcast(BF16).rearrange("p (k t) -> p k t", k=16)
        ys = sb("ys", [128, 4, 2048], F32)
        QTt = sb("QT", [128, 4096], BF16)
        qT = QTt[:, :].rearrange("p (h t) -> p h t", h=8)
        pf = QTt[:, 0:2048].bitcast(F32).rearrange("p (b c) -> p b c", b=4)
        pbb = QTt[:, 2048:3072].rearrange("p (b c) -> p b c", b=4)
        pT = QTt[:, 3072:4096].rearrange("p (c t) -> p c t", c=2)
        kT = sb("kT", [128, 2, 768], BF16)
        vv = sb("vv", [128, 6, 256], BF16)
        GUt = sb("GU", [128, 8192], BF16)
        gaT = GUt[:, 0:4096].rearrange("p (h t) -> p h t", h=8)
        uT = GUt[:, 4096:8192].rearrange("p (h t) -> p h t", h=8)
        x1T = GUt[:, :].rearrange("p (k t) -> p k t", k=16)
        NNt = sb("NN", [128, 4096], BF16)
        nn = NNt[:, :].rearrange("p (b c) -> p b c", b=4)
        sig = NNt[:, 0:2048].bitcast(F32).rearrange("p (s c) -> p s c", s=2)
        tmp2 = NNt[:, 2048:4096].bitcast(F32).rearrange("p (s c) -> p s c", s=2)
        PT = sb("PT", [128, 2, 3, 512], BF16)
        x1s = [x1b[:, 0, :], x1b[:, 1, :], PT[:, :, :, :].rearrange("p a b c -> p (a b c)")[:, 0:2048]]
        PT_ALL = [("PT", a_, b_) for a_ in range(2) for b_ in range(3)]
        RTt = sb("RT", [128, 2048], F32)
        Rr = RTt[:, 0:1024].rearrange("p (s c) -> p s c", s=2)
        tmpA = RTt[:, 1024:2048].rearrange("p (s c) -> p s c", s=2)
        g_bc = RTt[:, :]
        sg = sb("sg", [128, 2, 512], F32)
        ropeA = sb("ropeA", [128, 512], F32)
        ropeB = sb("ropeB", [128, 512], F32)
        qr = sb("qr", [128, 2, 512], BF16)
        junk = sb("junk", [128, 1024], BF16)
        postg = sb("postg", [128, 2048], F32)
        cs = sb("cs", [128, 6, 128], F32)
        Bias = sb("Bias", [128, 8, 128], F32)
        wsT = sb("wsT", [128, 8, 128], BF16)
        ident = sb("ident", [128, 128], BF16)
        ones = sb("ones", [128, 128], BF16)
        maskP = sb("maskP", [128, 128], BF16)
        maskN = sb("maskN", [128, 128], BF16)
        mf = sb("mf", [128, 128], F32)
        esink = sb("esink", [128, 8], F32)
        lng = sb("lng", [128, 8], F32)
        lnb = sb("lnb", [128, 8], F32)
        gk = sb("gk", [128, 16], F32)
        stt = sb("stt", [128, 64], F32)
        negh = sb("negh", [128, 4], F32)
        wr = sb("wr", [128, 2, 8192], BF16)
        wpe = sb("wpe", [128, 2, 1024], BF16)
        psum = [st.enter_context(nc.psum_tensor("ps%d" % i, [128, 512], F32)) for i in range(8)]
        psb = [p[:, :].bitcast(BF16) for p in psum]
        bank_ctr = [0]

        def bank():
            k = bank_ctr[0] % 8
            bank_ctr[0] += 1
            return k

        phase_cur = {}

        def phase(tok, ph):
            if phase_cur.get(tok) != ph:
                phase_cur[tok] = ph
                return [], [tok]
            return [tok], []

        C_SSQX, C_RX = 0, 6
        C_GSUM, C_GSSQ, C_GMEAN, C_GMSQ, C_GR = 12, 20, 24, 28, 32
        C_YSSQ, C_YT, C_YR = 36, 52, 56
        C_EPS = 60
        C_NH = 61

        def col(c):
            return stt[:, c:c + 1]

        def setup_dma(out, in_, w):
            S.add("sp", lambda e: e.dma_start(out=out, in_=in_), writes=[w], dma_key="setup")

        setup_dma(postg[:, :], post_g.partition_broadcast(128), "postg")
        setup_dma(esink[:, :], sink.partition_broadcast(128), "esink")
        bs_bc = tmpA[:, :, :].rearrange("p s c -> p (s c)")
        setup_dma(bs_bc, bs_d.partition_broadcast(128), ("tmpA", 0))
        for kc in range(16):
            setup_dma(gk[:, kc:kc + 1], pre_g[0:1, kc * 128:(kc + 1) * 128].rearrange("o p -> p o"), "gk")
        for h in range(8):
            setup_dma(lng[:, h:h + 1], ln_g[0:1, h * 128:(h + 1) * 128].rearrange("o p -> p o"), "lng")
            setup_dma(lnb[:, h:h + 1], ln_b[0:1, h * 128:(h + 1) * 128].rearrange("o p -> p o"), "lnb")
        wsf = sg[:, :, :].rearrange("p s (h q) -> p (s h) q", h=4)
        setup_dma(wsf, ws_d.rearrange("h p q -> p h q"), ("sg", 0))
        S.add("pool", lambda e: e.memset(mf[:, :], 1.0), writes=["mf"])
        S.add("pool", lambda e: e.affine_select(out=mf[:, :], in_=mf[:, :], pattern=[[-1, 128]], compare_op=ALU.is_equal,
                                                fill=0.0, base=0, channel_multiplier=1), reads=["mf"], writes=["mf"])
        S.add("pool", lambda e: e.tensor_copy(out=ident[:, :], in_=mf[:, :]), reads=["mf"], writes=["ident"])
        S.add("pool", lambda e: e.memset(mf[:, :], 1.0), reads=["mf"], writes=["mf"])
        S.add("pool", lambda e: e.affine_select(out=mf[:, :], in_=mf[:, :], pattern=[[-1, 128]], compare_op=ALU.is_ge,
                                                fill=0.0, base=0, channel_multiplier=1), reads=["mf"], writes=["mf"])
        S.add("pool", lambda e: e.tensor_copy(out=maskP[:, :], in_=mf[:, :]), reads=["mf"], writes=["maskP"])
        S.add("pool", lambda e: e.memset(mf[:, :], 1.0), reads=["mf"], writes=["mf"])
        S.add("pool", lambda e: e.affine_select(out=mf[:, :], in_=mf[:, :], pattern=[[1, 128]], compare_op=ALU.is_ge,
                                                fill=0.0, base=0, channel_multiplier=-1), reads=["mf"], writes=["mf"])
        S.add("pool", lambda e: e.tensor_copy(out=maskN[:, :], in_=mf[:, :]), reads=["mf"], writes=["maskN"])
        S.add("pool", lambda e: e.memset(ones[:, :], 1.0), writes=["ones"])
        S.add("pool", lambda e: e.memset(stt[:, C_EPS:C_EPS + 1], EPS), writes=["epsc"])
        S.add("pool", lambda e: e.memset(negh[:, :], -0.5), writes=["negh"])
        S.add("act", lambda e: e.activation(out=esink[:, :], in_=esink[:, :], func=AF.Exp), reads=["esink"], writes=["esink"])
        wsb = qr[:, :, :].rearrange("p s (h q) -> p (s h) q", h=4)
        S.add("dve", lambda e: e.tensor_copy(out=wsb, in_=wsf), reads=[("sg", 0)], writes=[("qr", 0)])
        k0 = bank()

        def f_wsT(e):
            for h in range(8):
                ins = e.transpose(out=psb[k0][:, h * 128:(h + 1) * 128], in_=wsb[:, h, :], identity=ident[:, :])
            return ins
        S.add("pe", f_wsT, reads=[("qr", 0), "ident"], writes=[("ps", k0)])
        S.add("act", lambda e: e.copy(out=wsT[:, :, :].rearrange("p h q -> p (h q)"), in_=psb[k0][:, :]),
              reads=[("ps", k0)], writes=["wsT"])
        for half in range(2):
            kk = bank()
            S.add("pe", lambda e, kk=kk, half=half: e.matmul(
                psum[kk][:, :], lhsT=ones[:, :], rhs=wsT[:, 4 * half:4 * half + 4, :].rearrange("p h q -> p (h q)"), start=True, stop=True),
                reads=["ones", "wsT"], writes=[("ps", kk)])
            for h4 in range(4):
                h = 4 * half + h4
                S.add("dve", lambda e, kk=kk, h=h, h4=h4: e.scalar_tensor_tensor(
                    out=Bias[:, h, :], in0=psum[kk][:, h4 * 128:(h4 + 1) * 128], scalar=lnb[:, h:h + 1],
                    in1=bs_bc[:, h * 128:(h + 1) * 128], op0=ALU.mult, op1=ALU.add),
                    reads=[("ps", kk), "lnb", ("tmpA", 0)], writes=[("Bias", h)])

        def mark(nm):
            if marks is not None:
                marks.append((nm, len(S.ops)))
        mark("setup_done")
        w_in_v = w_in.rearrange("(kc p) n -> p kc n", p=128)
        w_out_v = w_out.rearrange("(kc p) n -> p kc n", p=128)
        w_pg_v = w_pg.rearrange("(kc p) n -> p kc n", p=128)
        w_pe_v = w_pe.rearrange("(kc p) n -> p kc n", p=128)
        stF = [ys[:, 0, :], ys[:, 1, :], ys[:, 2, :]]
        _yb = ys[:, 3, :].bitcast(BF16)
        stB = [_yb[:, 0:2048], _yb[:, 2048:4096]]
        cj = [0, 0]
        conv_tok = {}

        def cv_engine_op(eng, o, i_, sc):
            if sc is None:
                if eng == "act":
                    return lambda e: e.copy(out=o, in_=i_)
                return lambda e: e.tensor_copy(out=o, in_=i_)

            def f(e):
                for kc in range(4):
                    oo, ii = o[:, kc * 512:(kc + 1) * 512], i_[:, kc * 512:(kc + 1) * 512]
                    if eng == "act":
                        ins = e.mul(out=oo, in_=ii, mul=sc[kc])
                    else:
                        ins = e.tensor_scalar(out=oo, in0=ii, scalar1=sc[kc], scalar2=None, op0=ALU.mult)
                return ins
            return f

        def conv_direct(p, slot):
            S.add("pool", lambda e: e.dma_start(out=wr[:, slot, :].rearrange("p (k c) -> p k c", k=16),
                                                in_=w_in_v[:, :, p * 512:(p + 1) * 512]),
                  writes=[("wr", slot)], dma_key=("wrc", slot))
            S.add("sp", lambda e: e.dma_start(out=win_s[p, :, :], in_=wr[:, slot, :]), reads=[("wr", slot)],
                  writes=[("win", p, "d")], dma_key=("wout_d", slot))
            conv_tok[("win", p)] = [("win", p, "d")]

        scratch_jobs = []
        for i in range(4):
            scratch_jobs.append((w_out_v[:, :, i * 512:(i + 1) * 512], wout_s[i, :, :], 16, ("wout", i)))
        for i in range(4):
            scratch_jobs.append((w_pe_v[:, :, i * 512:(i + 1) * 512], wpe_s[i, :, :], 2, ("wpe", i)))
            scratch_jobs.append((w_pg_v[:, :, i * 512:(i + 1) * 512], wpg_s[i, :, :], 16, ("wpg", i)))
        for (_, _, _, tok) in scratch_jobs:
            conv_tok[tok] = [tok + (0,)]
        for i in range(11):
            conv_tok[("win", i)] = [("win", i, "d")]

        def conv_scratch(n):
            for _ in range(n):
                if not scratch_jobs:
                    return
                src, dst, nk, tok = scratch_jobs.pop(0)
                S.add("pool", lambda e, src=src, dst=dst, nk=nk: e.dma_start(
                    out=dst.rearrange("p (k c) -> p k c", k=nk), in_=src),
                    writes=[tok + (0,)], dma_key=("cv",) + tok)

        first_use_extra = {"qr": [("qr", 0)], "sg1": [("sg", 0)], "tmpA1": [("tmpA", 0)]}

        tiles = []
        for (which, row0, nblk) in seqs:
            for b0 in range(0, nblk, NB):
                hl = 1 if b0 > 0 else 0
                hr = 1 if b0 + NB < nblk else 0
                tiles.append(dict(which=which, row0=row0, b0=b0, hl=hl, hr=hr, nkv=NB + hl + hr,
                                  prev_hl=(tiles[-1]["hl"] if hl else 0)))

        xh_ctr = [0]

        def stage0(T, as_list=False):
            out_list = []
            nkv, hl = T["nkv"], T["hl"]
            xsrc = xin[T["which"]]
            pos0 = (T["b0"] - hl) * 128
            def f_cs():
                S.add("sp", lambda e: e.dma_start(out=g_bc, in_=pre_g.partition_broadcast(128)),
                      writes=["gbc", ("Rr", 0), ("Rr", 1), ("tmpA", 0), ("tmpA", 1)], dma_key="gbc")
                S.add("sp", lambda e: e.dma_start(out=cs[:, 0:nkv, :],
                                                  in_=cs_d[pos0:pos0 + nkv * 128, :].rearrange("(s p) c -> p s c", p=128)),
                      writes=["cs"], dma_key="cs")
            out_list.append(f_cs)
            for s in range(hl, nkv):
                out_list.append(lambda s=s: s0_block(T, s, xsrc, pos0))
            if as_list:
                return out_list
            for f in out_list:
                pb_ = f()
                if pb_ is not None:
                    pb_()

        def s0_block(T, s, xsrc, pos0):
            if True:
                i = xh_ctr[0] % 2
                xh_ctr[0] += 1
                r0 = T["row0"] + pos0 + s * 128
                pr, pw = phase("XH", ("x", id(T)))
                S.add("sp", lambda e, i=i, r0=r0: e.dma_start(out=xh[:, i, :], in_=xsrc[r0:r0 + 128, :]),
                      reads=pr, writes=[("xh", i)] + pw + first_use_extra.pop("XH", []), dma_key=("xh", i))
                S.add("act", lambda e, i=i, s=s: e.activation(out=hb[:, i, :], in_=xh[:, i, :], func=AF.Square,
                                                              accum_out=col(C_SSQX + s)),
                      reads=[("xh", i), "XH"], writes=[("hb", i), ("st", C_SSQX + s)])
                S.add("pool", lambda e, s=s: e.tensor_scalar(out=col(C_RX + s), in0=col(C_SSQX + s), scalar1=1.0 / D,
                                                             scalar2=EPS, op0=ALU.mult, op1=ALU.add),
                      reads=[("st", C_SSQX + s)], writes=[("st", C_RX + s)])
                S.add("pool", lambda e, s=s: e.tensor_tensor(out=col(C_RX + s), in0=col(C_RX + s), in1=negh[:, 0:1], op=ALU.pow),
                      reads=[("st", C_RX + s), "negh"], writes=[("st", C_RX + s)])
                S.add("dve", lambda e, i=i, s=s: e.scalar_tensor_tensor(out=hb[:, i, :], in0=xh[:, i, :], scalar=col(C_RX + s),
                                                                        in1=g_bc, op0=ALU.mult, op1=ALU.mult),
                      reads=[("xh", i), ("st", C_RX + s), "XH", "gbc", ("Rr", 0), ("Rr", 1), ("tmpA", 0), ("tmpA", 1)],
                      writes=[("hb", i)])

            def partB(i=i, s=s):
                for half in range(2):
                    k = bank()

                    def f_tr(e, i=i, half=half, k=k):
                        for c in range(8):
                            ins = e.transpose(out=psb[k][:, c * 128:(c + 1) * 128],
                                              in_=hb[:, i, (half * 8 + c) * 128:(half * 8 + c + 1) * 128], identity=ident[:, :])
                        return ins
                    S.add("pe", f_tr, reads=[("hb", i), "ident"], writes=[("ps", k)])
                    pr, pw = phase("RB", ("h", id(T)))
                    eng = "act" if half == 0 else "dve"

                    def f_cp(e, k=k, half=half, s=s, eng=eng):
                        o = hT[:, half * 8:half * 8 + 8, s * 128:(s + 1) * 128]
                        i_ = psb[k][:, :].rearrange("p (c t) -> p c t", c=8)
                        return e.copy(out=o, in_=i_) if eng == "act" else e.tensor_copy(out=o, in_=i_)
                    S.add(eng, f_cp, reads=[("ps", k)] + pr, writes=[("hT", s)] + pw)
            return partB

        wr_ctr = [0]

        def load_piece(src, toks):
            s = wr_ctr[0] % 2
            wr_ctr[0] += 1
            S.add("sp", lambda e: e.dma_start(out=wr[:, s, :], in_=src), reads=toks, writes=[("wr", s)], dma_key=("wr", s))
            return s

        deferred = []

        def run_deferred():
            n = len(deferred)
            for _ in range(n):
                deferred.pop(0)()

        def mm_tok(T, ws, k, slot, ncols=512, c0=0):
            def f(e):
                for kc in range(16):
                    ins = e.matmul(psum[k][:, 0:ncols], lhsT=hT[:, kc, slot * 128:(slot + 1) * 128],
                                   rhs=wr[:, ws, kc * 512 + c0:kc * 512 + c0 + ncols], start=(kc == 0), stop=(kc == 15))
                return ins
            S.add("pe", f, reads=[("wr", ws), ("hT", slot), "RB"], writes=[("ps", k)])
            run_deferred()

        def mm_feat(T, ws, k, cg):
            hl = T["hl"]

            def f(e):
                for kc in range(16):
                    ins = e.matmul(psum[k][:, :], lhsT=wr[:, ws, kc * 512 + cg * 128:kc * 512 + (cg + 1) * 128],
                                   rhs=hT[:, kc, hl * 128:hl * 128 + 512], start=(kc == 0), stop=(kc == 15))
                return ins
            S.add("pe", f, reads=[("wr", ws), "RB"] + [("hT", hl + b) for b in range(NB)], writes=[("ps", k)])
            run_deferred()

        def rope_evac(k, nh, slot, dst_fn, dst_tok_r, dst_tok_w):
            w = nh * 128
            psv = psum[k][:, 0:w].rearrange("p (h t d) -> p h t d", h=nh, t=2)
            Av = ropeA[:, 0:w].rearrange("p (h t d) -> p h t d", h=nh, t=2)
            Bv = ropeB[:, 0:w].rearrange("p (h t d) -> p h t d", h=nh, t=2)
            qs = slot % 2
            qv = qr[:, qs, 0:w].rearrange("p (h t d) -> p h t d", h=nh, t=2)
            cosb = cs[:, slot, 0:64]
            sinb = cs[:, slot, 64:128]
            S.add("dve", lambda e: e.tensor_tensor(out=Av, in0=psv,
                                                   in1=cosb.unsqueeze(1).unsqueeze(1).to_broadcast([128, nh, 2, 64]), op=ALU.mult),
                  reads=[("ps", k), "cs"], writes=["ropeA"])
            S.add("dve", lambda e: e.tensor_tensor(out=Bv[:, :, 0, :], in0=psv[:, :, 1, :],
                                                   in1=sinb.unsqueeze(1).to_broadcast([128, nh, 64]), op=ALU.mult),
                  reads=[("ps", k), "cs"], writes=["ropeB0"])
            S.add("dve", lambda e: e.tensor_tensor(out=Bv[:, :, 1, :], in0=psv[:, :, 0, :],
                                                   in1=sinb.unsqueeze(1).to_broadcast([128, nh, 64]), op=ALU.mult),
                  reads=[("ps", k), "cs"], writes=["ropeB1"])
            S.add("pool", lambda e: e.tensor_tensor(out=qv[:, :, 0, :], in0=Av[:, :, 0, :], in1=Bv[:, :, 0, :], op=ALU.subtract),
                  reads=["ropeA", "ropeB0"], writes=[("qr", qs, 0)] + first_use_extra.pop("qr", []))
            S.add("pool", lambda e: e.tensor_tensor(out=qv[:, :, 1, :], in0=Av[:, :, 1, :], in1=Bv[:, :, 1, :], op=ALU.add),
                  reads=["ropeA", "ropeB1"], writes=[("qr", qs, 1)])
            def part2():
                k2 = bank()

                def f_tr(e):
                    for h in range(nh):
                        ins = e.transpose(out=psb[k2][:, h * 128:(h + 1) * 128], in_=qr[:, qs, h * 128:(h + 1) * 128],
                                          identity=ident[:, :])
                    return ins
                S.add("pe", f_tr, reads=[("qr", qs, 0), ("qr", qs, 1), ("qr", 0), "ident"], writes=[("ps", k2)])
                S.add("act", lambda e: e.copy(out=dst_fn(), in_=psb[k2][:, 0:w].rearrange("p (h t) -> p h t", h=nh)),
                      reads=[("ps", k2)] + dst_tok_r, writes=dst_tok_w)
            deferred.append(part2)

        def piece_kv(T, ws):
            if T["hl"]:
                ps_ = T["prev_hl"] + NB - 1
                for d_, s_ in ((0, ps_), (1, ps_ + 1)):
                    S.add("pool", lambda e, d_=d_, s_=s_: e.tensor_copy(out=kT[:, :, d_ * 128:(d_ + 1) * 128],
                                                                        in_=kT[:, :, s_ * 128:(s_ + 1) * 128]),
                          reads=[("kT", s_)], writes=[("kT", d_)])
                    S.add("pool", lambda e, d_=d_, s_=s_: e.tensor_copy(out=vv[:, d_, :], in_=vv[:, s_, :]),
                          reads=[("vv", s_)], writes=[("vv", d_)])
            for s in range(2 * T["hl"], T["nkv"]):
                k = bank()
                mm_tok(T, ws, k, s)
                rope_evac(k, 2, s, lambda s=s: kT[:, :, s * 128:(s + 1) * 128], [], [("kT", s)])
                S.add("dve", lambda e, k=k, s=s: e.tensor_copy(out=vv[:, s, :], in_=psum[k][:, 256:512]),
                      reads=[("ps", k)], writes=[("vv", s)])

        def piece_q(T, ws, g):
            for b in range(NB):
                k = bank()
                mm_tok(T, ws, k, T["hl"] + b)
                pr, pw = phase("QT", ("q", id(T)))
                rope_evac(k, 4, T["hl"] + b, lambda b=b: qT[:, 4 * g:4 * g + 4, b * 128:(b + 1) * 128], pr, [("qT", g, b)] + pw)

        def piece_gate(T, ws, which, half, hooks=None):
            for cg in range(4):
                if hooks and cg > 0 and hooks[cg - 1] is not None:
                    hooks[cg - 1]()
                k = bank()
                mm_feat(T, ws, k, cg)
                h = 4 * half + cg
                if which == "ga":
                    pr, pw = phase("GU", ("g", id(T)))
                    ex = first_use_extra.pop("GU", [])
                    S.add("act", lambda e, k=k, h=h: e.activation(out=gaT[:, h, :], in_=psum[k][:, :], func=AF.Silu),
                          reads=[("ps", k)] + pr, writes=[("gaT", h)] + pw + ex)
                elif which == "u":
                    pr, pw = phase("GU", ("g", id(T)))
                    S.add("act", lambda e, k=k, h=h: e.activation(out=uT[:, h, :], in_=psum[k][:, :], func=AF.Gelu),
                          reads=[("ps", k)] + pr, writes=[("uT", h)] + pw)
                else:
                    s2 = cg % 2
                    S.add("act", lambda e, k=k, s2=s2: e.activation(out=sg[:, s2, :], in_=psum[k][:, :], func=AF.Silu),
                          reads=[("ps", k)], writes=[("sg", s2)] + (first_use_extra.pop("sg1", []) if s2 == 1 else []))
                    S.add("dve", lambda e, h=h, s2=s2: e.tensor_tensor(out=uT[:, h, :], in0=uT[:, h, :], in1=sg[:, s2, :],
                                                                       op=ALU.mult),
                          reads=[("sg", s2), ("uT", h), "GU"], writes=[("uT", h)])
            if hooks and hooks[3] is not None:
                hooks[3]()

        def piece_vg(T, ws, half):
            for b in range(NB):
                k = bank()
                mm_tok(T, ws, k, T["hl"] + b)
                pr, pw = phase("XH", ("gv", id(T)))
                S.add("act", lambda e, k=k, b=b: e.activation(out=gv[:, b, half * 512:(half + 1) * 512], in_=psum[k][:, :],
                                                              func=AF.Gelu, accum_out=col(C_GSUM + 2 * b + half)),
                      reads=[("ps", k)] + pr, writes=[("gv", b, half), ("st", C_GSUM + 2 * b + half)] + pw)
                if half == 1:
                    S.add("act", lambda e, b=b: e.activation(out=junk[:, :], in_=gv[:, b, :], func=AF.Square,
                                                             accum_out=col(C_GSSQ + b)),
                          reads=[("gv", b, 0), ("gv", b, 1), "XH"], writes=[("st", C_GSSQ + b), ("junk", 0), ("junk", 1)])
            if half == 1:
                gs = stt[:, C_GSUM:C_GSUM + 8].rearrange("p (b t) -> p b t", t=2)
                mean = stt[:, C_GMEAN:C_GMEAN + 4]
                msq = stt[:, C_GMSQ:C_GMSQ + 4]
                gr = stt[:, C_GR:C_GR + 4]
                gssq = stt[:, C_GSSQ:C_GSSQ + 4]
                rs = [("st", C_GSUM + i) for i in range(8)]
                S.add("dve", lambda e: e.tensor_tensor(out=mean, in0=gs[:, :, 0], in1=gs[:, :, 1], op=ALU.add),
                      reads=rs, writes=["gmean"])
                S.add("dve", lambda e: e.tensor_scalar(out=mean, in0=mean, scalar1=1.0 / 1024, scalar2=None, op0=ALU.mult),
                      reads=["gmean"], writes=["gmean"])
                S.add("dve", lambda e: e.tensor_tensor(out=msq, in0=mean, in1=mean, op=ALU.mult), reads=["gmean"], writes=["gmsq"])
                S.add("dve", lambda e: e.scalar_tensor_tensor(out=gr, in0=gssq, scalar=1.0 / 1024, in1=msq, op0=ALU.mult,
                                                              op1=ALU.subtract),
                      reads=["gmsq"] + [("st", C_GSSQ + b) for b in range(4)], writes=["gr"])
                S.add("dve", lambda e: e.tensor_scalar(out=gr, in0=gr, scalar1=EPS, scalar2=None, op0=ALU.add),
                      reads=["gr"], writes=["gr"])
                S.add("pool", lambda e: e.tensor_tensor(out=gr, in0=gr, in1=negh[:, 0:4], op=ALU.pow),
                      reads=["gr", "negh"], writes=["gr"])
                for b in range(NB):
                    pr, pw = phase("NN", ("n", id(T)))
                    S.add("dve", lambda e, b=b: e.tensor_scalar(out=nn[:, b, :], in0=gv[:, b, :],
                                                                 scalar1=stt[:, C_GMEAN + b:C_GMEAN + b + 1],
                                                                 scalar2=stt[:, C_GR + b:C_GR + b + 1],
                                                                 op0=ALU.subtract, op1=ALU.mult),
                          reads=[("gv", b, 0), ("gv", b, 1), "gmean", "gr", "XH"] + pr,
                          writes=[("nn", b)] + pw + first_use_extra.pop("NN", []))

        def att_units(T):
            nblk_seq = None
            return [(n, g) for n in range(NB) for g in range(2)]

        att_state = {}

        def att_qk(T, u, ui):
            n, g = u
            hl, nkv = T["hl"], T["nkv"]
            own = hl + n
            js = [s for s in (own - 1, own, own + 1) if 0 <= s < nkv]
            ps_ = ui % 2
            banks = []
            for jj, s in enumerate(js):
                k = bank()
                banks.append(k)
                S.add("pe", lambda e, k=k, s=s: e.matmul(psum[k][:, :].rearrange("p (h q) -> p h q", h=4), lhsT=kT[:, g, s * 128:(s + 1) * 128],
                                                         rhs=qT[:, 4 * g:4 * g + 4, n * 128:(n + 1) * 128], start=True, stop=True),
                      reads=[("kT", s), ("qT", g, n), "QT"], writes=[("ps", k)])
                S.add("act", lambda e, k=k, jj=jj: e.activation(out=PT[:, ps_, jj, :], in_=psum[k][:, :], func=AF.Exp, scale=SCALE),
                      reads=[("ps", k)], writes=[("PT", ps_, jj)])
                if s != own:
                    m = maskP if s < own else maskN
                    S.add("pool", lambda e, jj=jj, m=m: e.tensor_tensor(
                        out=PT[:, ps_, jj, :].rearrange("p (h q) -> p h q", h=4),
                        in0=PT[:, ps_, jj, :].rearrange("p (h q) -> p h q", h=4),
                        in1=m[:, :].unsqueeze(1).to_broadcast([128, 4, 128]), op=ALU.mult),
                        reads=[("PT", ps_, jj), "maskP", "maskN"], writes=[("PT", ps_, jj)])
            att_state[ui] = js

        def att_pv(T, u, ui):
            n, g = u
            js = att_state.pop(ui)
            ps_ = ui % 2
            kO, kD = bank(), bank()

            def f_pv(e):
                for jj, s in enumerate(js):
                    ins = e.matmul(psum[kO][:, :], lhsT=vv[:, s, g * 128:(g + 1) * 128], rhs=PT[:, ps_, jj, :],
                                   start=(jj == 0), stop=(jj == len(js) - 1))
                return ins

            def f_d(e):
                for jj, s in enumerate(js):
                    ins = e.matmul(psum[kD][:, :], lhsT=ones[:, :], rhs=PT[:, ps_, jj, :],
                                   start=(jj == 0), stop=(jj == len(js) - 1))
                return ins
            ptr = [("PT", ps_, jj) for jj in range(len(js))]
            S.add("pe", f_pv, reads=ptr + [("vv", s) for s in js], writes=[("ps", kO)])
            S.add("pe", f_d, reads=ptr + ["ones"], writes=[("ps", kD)])
            rs = ui % 2

            def f_r(e):
                for h in range(4):
                    ins = e.tensor_scalar(out=Rr[:, rs, h * 128:(h + 1) * 128], in0=psum[kD][:, h * 128:(h + 1) * 128],
                                          scalar1=esink[:, 4 * g + h:4 * g + h + 1], scalar2=None, op0=ALU.add)
                return ins
            S.add("dve", f_r, reads=[("ps", kD), "esink"], writes=[("Rr", rs)])
            S.add("dve", lambda e: e.reciprocal(out=Rr[:, rs, :], in_=Rr[:, rs, :]), reads=[("Rr", rs)], writes=[("Rr", rs)])
            S.add("dve", lambda e: e.tensor_tensor(out=tmpA[:, rs, :], in0=psum[kO][:, :], in1=Rr[:, rs, :], op=ALU.mult),
                  reads=[("ps", kO), ("Rr", rs)], writes=[("tmpA", rs)] + (first_use_extra.pop("tmpA1", []) if rs == 1 else []))
            pr, pw = phase("XH", ("am", id(T)))
            S.add("dve", lambda e: e.tensor_tensor(out=amT[:, 4 * g:4 * g + 4, n * 128:(n + 1) * 128],
                                                    in0=tmpA[:, rs, :].rearrange("p (h q) -> p h q", h=4),
                                                    in1=gaT[:, 4 * g:4 * g + 4, n * 128:(n + 1) * 128], op=ALU.mult),
                  reads=[("tmpA", rs), "GU"] + [("gaT", 4 * g + h) for h in range(4)] + pr,
                  writes=[("amT", 0, n, g)] + pw)

        sp_ctr = [0]

        def spatial(T, b, hq):
            k = bank()
            rs = sp_ctr[0] % 2
            sp_ctr[0] += 1

            def f(e):
                for h in range(4):
                    hh = 4 * hq + h
                    ins = e.matmul(psum[k][:, h * 128:(h + 1) * 128], lhsT=nn[:, b, hh * 128:(hh + 1) * 128],
                                   rhs=wsT[:, hh, :], start=True, stop=True)
                return ins
            S.add("pe", f, reads=[("nn", b), "wsT", "NN"], writes=[("ps", k)])

            def f2(e):
                for h in range(4):
                    hh = 4 * hq + h
                    ins = e.scalar_tensor_tensor(out=tmpA[:, rs, h * 128:(h + 1) * 128], in0=psum[k][:, h * 128:(h + 1) * 128],
                                                 scalar=lng[:, hh:hh + 1], in1=Bias[:, hh, :], op0=ALU.mult, op1=ALU.add)
                return ins
            S.add("dve", f2, reads=[("ps", k), "lng"] + [("Bias", 4 * hq + h) for h in range(4)],
                  writes=[("tmpA", rs)] + (first_use_extra.pop("tmpA1", []) if rs == 1 else []))
            pr, pw = phase("XH", ("am", id(T)))
            S.add("dve", lambda e: e.tensor_tensor(out=amT[:, 8 + 4 * hq:8 + 4 * hq + 4, b * 128:(b + 1) * 128],
                                                    in0=tmpA[:, rs, :].rearrange("p (h q) -> p h q", h=4),
                                                    in1=uT[:, 4 * hq:4 * hq + 4, b * 128:(b + 1) * 128], op=ALU.mult),
                  reads=[("tmpA", rs), "GU"] + [("uT", 4 * hq + h) for h in range(4)] + pr,
                  writes=[("amT", 1, b, hq)] + pw)

        def piece_wout(T, ws, i):
            if i == 0:
                x_reload(T)
            if i == 0:
                p_prep(T)
            if i == 3:
                p_prep_b(T)
            if i == 2:
                T["_wps"][0] = load_wpe(0)
            for b in range(NB):
                k = bank()

                def f(e, k=k, b=b):
                    for fc in range(16):
                        ins = e.matmul(psum[k][:, :], lhsT=amT[:, fc, b * 128:(b + 1) * 128],
                                       rhs=wr[:, ws, fc * 512:(fc + 1) * 512], start=(fc == 0), stop=(fc == 15))
                    return ins
                S.add("pe", f, reads=[("wr", ws), "XH"] + [("amT", 0, b, g) for g in range(2)] + [("amT", 1, b, g) for g in range(2)],
                      writes=[("ps", k)])
                ex = first_use_extra.pop("ys", [])
                S.add("act", lambda e, k=k, b=b: e.copy(out=ys[:, b, i * 512:(i + 1) * 512], in_=psum[k][:, :]),
                      reads=[("ps", k)], writes=[("ys", b, i)] + ex)
                js = b % 2
                S.add("act", lambda e, k=k, b=b, js=js: e.activation(out=junk[:, js * 512:(js + 1) * 512], in_=psum[k][:, :],
                                                                     func=AF.Square, accum_out=col(C_YSSQ + 4 * b + i)),
                      reads=[("ps", k)], writes=[("st", C_YSSQ + 4 * b + i), ("junk", js)])
                if i == 3:
                    if b >= 3:
                        x1_block_b(T, b - 3)
                    x1_block_a(T, b)
            if i == 3:
                T["_x1last"] = [lambda: x1_block_b(T, 1), lambda: x1_block_b(T, 2), lambda: x1_block_b(T, 3)]

        def x_reload(T):
            xsrc = xin[T["which"]]
            for b in range(NB):
                r0 = T["row0"] + (T["b0"] + b) * 128
                if b < 3:
                    pr, pw = phase("RB", ("xr", id(T)))
                    S.add("sp", lambda e, b=b, r0=r0: e.dma_start(out=xrl[b], in_=xsrc[r0:r0 + 128, :]), reads=pr,
                          writes=[("xrl", b)] + pw, dma_key=("xrl", b))
                else:
                    S.add("sp", lambda e, b=b, r0=r0: e.dma_start(out=xrl[b], in_=xsrc[r0:r0 + 128, :]),
                          writes=[("xrl", b), ("hb", 0), ("hb", 1)], dma_key=("xrl", b))

        def x1_block_a(T, b):
            ysb = [("ys", b, i_) for i_ in range(4)]
            yt, yr = col(C_YT + b), col(C_YR + b)
            c_ = [col(C_YSSQ + 4 * b + j) for j in range(4)]
            S.add("pool", lambda e: e.tensor_scalar(out=yt, in0=c_[0], scalar1=c_[1], scalar2=c_[2], op0=ALU.add, op1=ALU.add),
                  reads=[("st", C_YSSQ + 4 * b + j) for j in range(4)], writes=[("st", C_YT + b)])
            S.add("pool", lambda e: e.tensor_scalar(out=yt, in0=yt, scalar1=c_[3], scalar2=None, op0=ALU.add),
                  reads=[("st", C_YSSQ + 4 * b + 3), ("st", C_YT + b)], writes=[("st", C_YT + b)])
            S.add("pool", lambda e: e.tensor_scalar(out=yr, in0=yt, scalar1=1.0 / D, scalar2=EPS, op0=ALU.mult, op1=ALU.add),
                  reads=[("st", C_YT + b)], writes=[("st", C_YR + b)])
            S.add("pool", lambda e: e.tensor_tensor(out=yr, in0=yr, in1=negh[:, 0:1], op=ALU.pow),
                  reads=[("st", C_YR + b), "negh"], writes=[("st", C_YR + b)])
            S.add("dve", lambda e: e.scalar_tensor_tensor(out=ys[:, b, :], in0=ys[:, b, :], scalar=yr, in1=postg[:, :],
                                                          op0=ALU.mult, op1=ALU.mult),
                  reads=ysb + [("st", C_YR + b), "postg"], writes=ysb)
            S.add("dve", lambda e: e.tensor_tensor(out=ys[:, b, :], in0=ys[:, b, :], in1=xrl[b], op=ALU.add),
                  reads=ysb + [("xrl", b)] + (["RB"] if b < 3 else [("hb", 0), ("hb", 1)]), writes=ysb)
            S.add("act", lambda e: e.copy(out=x1s[b % 3], in_=ys[:, b, :]), reads=ysb,
                  writes=[("x1b", b % 3)] + (PT_ALL if b % 3 == 2 else []))

        def x1_block_b(T, b):
            for half in range(2):
                k = bank()

                def f_tr(e, half=half, k=k, b=b):
                    for c in range(8):
                        ins = e.transpose(out=psb[k][:, c * 128:(c + 1) * 128],
                                          in_=x1s[b % 3][:, (half * 8 + c) * 128:(half * 8 + c + 1) * 128],
                                          identity=ident[:, :])
                    return ins
                S.add("pe", f_tr, reads=[("x1b", b % 3), "ident"] + (PT_ALL if b % 3 == 2 else []), writes=[("ps", k)])
                pr, pw = phase("GU", ("x", id(T)))

                eng = "act" if half == 0 else "dve"

                def f_cp(e, k=k, half=half, b=b, eng=eng):
                    o = x1T[:, half * 8:half * 8 + 8, b * 128:(b + 1) * 128]
                    i_ = psb[k][:, :].rearrange("p (c t) -> p c t", c=8)
                    return e.copy(out=o, in_=i_) if eng == "act" else e.tensor_copy(out=o, in_=i_)
                S.add(eng, f_cp, reads=[("ps", k)] + pr, writes=[("x1T", b, half)] + pw)

        def p_prep(T):
            psrc = pin[T["which"]]
            r0 = T["row0"] + T["b0"] * 128
            pr, pw = phase("QT", ("p", id(T)))
            S.add("sp", lambda e: e.dma_start(out=pf, in_=psrc[r0:r0 + NB * 128, :].rearrange("(b p) c -> p b c", p=128)),
                  reads=pr, writes=["pf"] + pw, dma_key="pf")
            S.add("dve", lambda e: e.tensor_copy(out=pbb, in_=pf), reads=["pf", "QT"], writes=["pbb"])

        def p_prep_b(T):
            k = bank()

            def f_tr(e):
                for b in range(NB):
                    for c in range(2):
                        ins = e.transpose(out=psb[k][:, (c * 4 + b) * 128:(c * 4 + b + 1) * 128],
                                          in_=pbb[:, b, c * 128:(c + 1) * 128], identity=ident[:, :])
                return ins
            S.add("pe", f_tr, reads=["pbb", "ident", "QT"], writes=[("ps", k)])
            S.add("act", lambda e: e.copy(out=pT.rearrange("p c t -> p (c t)"), in_=psb[k][:, :]),
                  reads=[("ps", k), "QT"], writes=["pT"])

        wpe_ctr = [0]

        def load_wpe(i):
            s = wpe_ctr[0] % 2
            wpe_ctr[0] += 1
            S.add("sp", lambda e: e.dma_start(out=wpe[:, s, :], in_=wpe_s[i, :, :]), reads=conv_tok[("wpe", i)],
                  writes=[("wpe", s)], dma_key=("wpe", s))
            return s

        s5_ctr = [0]

        def piece_pg(T, ws, wps, i):
            yd = yout[T["which"]]
            for b in range(NB):
                kA, kB = bank(), bank()

                def fA(e, kA=kA, b=b):
                    for kc in range(16):
                        ins = e.matmul(psum[kA][:, :], lhsT=x1T[:, kc, b * 128:(b + 1) * 128],
                                       rhs=wr[:, ws, kc * 512:(kc + 1) * 512], start=(kc == 0), stop=(kc == 15))
                    return ins

                def fB(e, kB=kB, b=b):
                    for c in range(2):
                        ins = e.matmul(psum[kB][:, :], lhsT=pT[:, c, b * 128:(b + 1) * 128],
                                       rhs=wpe[:, wps, c * 512:(c + 1) * 512], start=(c == 0), stop=(c == 1))
                    return ins
                S.add("pe", fA, reads=[("wr", ws), ("x1T", b, 0), ("x1T", b, 1), "GU"], writes=[("ps", kA)])
                S.add("pe", fB, reads=[("wpe", wps), "pT", "QT"], writes=[("ps", kB)])
                s = s5_ctr[0] % 2
                s5_ctr[0] += 1
                pr, pw = phase("NN", ("s", id(T)))
                S.add("act", lambda e, kA=kA, s=s: e.activation(out=sig[:, s, :], in_=psum[kA][:, :], func=AF.Sigmoid),
                      reads=[("ps", kA)] + pr, writes=[("sig", s)] + pw)
                S.add("dve", lambda e, kB=kB, s=s: e.tensor_tensor(out=tmp2[:, s, :], in0=psum[kB][:, :], in1=sig[:, s, :],
                                                                  op=ALU.mult),
                      reads=[("ps", kB), ("sig", s), "NN"], writes=[("tmp2", s)])
                S.add("dve", lambda e, b=b, s=s: e.tensor_tensor(out=ys[:, b, i * 512:(i + 1) * 512],
                                                                 in0=ys[:, b, i * 512:(i + 1) * 512], in1=tmp2[:, s, :], op=ALU.add),
                      reads=[("tmp2", s), ("ys", b, i), "NN"], writes=[("ys", b, i)])
                gidx = i * NB + b
                if gidx <= 2 and T.get("_x1last"):
                    T["_x1last"].pop(0)()
                pend = T.setdefault("_s0pend", [])
                if gidx % 2 == 0:
                    if len(pend) >= 2:
                        pend.pop(0)()
                    if T.get("_s0next"):
                        if gidx == 0:
                            T["_s0next"].pop(0)()
                        if T["_s0next"]:
                            pend.append(T["_s0next"].pop(0)())
                if i == 3 and b == NB - 1:
                    while pend:
                        pend.pop(0)()
                    while T.get("_s0next"):
                        T["_s0next"].pop(0)()()
                if i == 3:
                    r0 = T["row0"] + (T["b0"] + b) * 128
                    S.add("sp", lambda e, b=b, r0=r0: e.dma_start(out=yd[r0:r0 + 128, :], in_=ys[:, b, :]),
                          reads=[("ys", b, i_) for i_ in range(4)], dma_key=("o", b))

        plan = []
        for ti, T in enumerate(tiles):
            T["_wps"] = {}

            def add_piece(src, toks, fn):
                plan.append((src, toks, fn))

            units = [(n, g) for g in range(2) for n in range(NB)]
            add_piece(win_s[2], conv_tok[("win", 2)], lambda ws, T=T: piece_kv(T, ws))
            add_piece(win_s[0], conv_tok[("win", 0)], lambda ws, T=T: piece_q(T, ws, 0))
            add_piece(win_s[1], conv_tok[("win", 1)], lambda ws, T=T: piece_q(T, ws, 1))
            add_piece(win_s[7], conv_tok[("win", 7)], lambda ws, T=T: piece_vg(T, ws, 0))
            add_piece(win_s[8], conv_tok[("win", 8)], lambda ws, T=T: piece_vg(T, ws, 1))

            def hk(pv_list, qk_list, T=T, units=units):
                h = [None, None, None, None]
                for j, ui in enumerate(pv_list):
                    h[2 * j] = (lambda ui=ui: att_pv(T, units[ui], ui))
                for j, ui in enumerate(qk_list):
                    h[2 * j + 1] = (lambda ui=ui: att_qk(T, units[ui], ui))
                return h
            add_piece(win_s[3], conv_tok[("win", 3)], lambda ws, T=T, hk=hk: piece_gate(T, ws, "ga", 0, hk([], [0, 1])))
            add_piece(win_s[4], conv_tok[("win", 4)], lambda ws, T=T, hk=hk: piece_gate(T, ws, "ga", 1, hk([0, 1], [2, 3])))
            add_piece(win_s[5], conv_tok[("win", 5)], lambda ws, T=T, hk=hk: piece_gate(T, ws, "u", 0, hk([2, 3], [4, 5])))
            add_piece(win_s[6], conv_tok[("win", 6)], lambda ws, T=T, hk=hk: piece_gate(T, ws, "u", 1, hk([4, 5], [6, 7])))
            add_piece(win_s[9], conv_tok[("win", 9)], lambda ws, T=T, hk=hk: piece_gate(T, ws, "gg", 0, hk([6, 7], [])))
            add_piece(win_s[10], conv_tok[("win", 10)], lambda ws, T=T: piece_gate(T, ws, "gg", 1))

            def p_sp(ws, T=T, units=units):
                for b in range(4):
                    for hq in range(2):
                        spatial(T, b, hq)
            add_piece(None, [], p_sp)
            for i in range(4):
                add_piece(wout_s[i], conv_tok[("wout", i)], lambda ws, T=T, i=i: piece_wout(T, ws, i))

            def p_x1(ws, T=T, ti=ti):

                T["_s0next"] = stage0(tiles[ti + 1], as_list=True) if ti + 1 < len(tiles) else []
            add_piece(None, [], p_x1)

            def p_pg(ws, T=T, i=0):
                if i + 1 < 4:
                    T["_wps"][i + 1] = load_wpe(i + 1)
                piece_pg(T, ws, T["_wps"][i], i)
            for i in range(4):
                add_piece(wpg_s[i], conv_tok[("wpg", i)], lambda ws, T=T, i=i, p_pg=p_pg: p_pg(ws, T, i))

        stage0(tiles[0])
        real = [pi for pi, p in enumerate(plan) if p[0] is not None]
        nxt_real = {}
        for a, b_ in zip(real[:-1], real[1:]):
            nxt_real[a] = b_
        loaded = {}

        n_t0 = len(plan) // len(tiles)
        win_order = [2, 0, 1, 7, 8, 3, 4, 5, 6, 9, 10]

        def ensure_loaded(pi):
            if pi not in loaded:
                if pi < 11:
                    slot = wr_ctr[0] % 2
                    wr_ctr[0] += 1
                    conv_direct(win_order[pi], slot)
                    loaded[pi] = slot
                else:
                    loaded[pi] = load_piece(plan[pi][0], plan[pi][1])

        mark("stage0_done")
        ensure_loaded(real[0])
        for pi, (src, toks, fn) in enumerate(plan):
            mark("plan%d" % pi)
            if src is not None:
                ensure_loaded(pi)
                if pi in nxt_real and nxt_real[pi] < 11:
                    ensure_loaded(nxt_real[pi])
                if pi < 11:
                    conv_scratch(2 if pi < 2 else 1)
                if pi in nxt_real:
                    ensure_loaded(nxt_real[pi])
                fn(loaded[pi])
            else:
                fn(None)

        if max_ops is not None:
            S.ops = S.ops[:max_ops]
        S.emit(nc, st)
    return nc


def _cs_table():
    inv = 1.0 / (10000.0 ** (np.arange(0, 128, 2, dtype=np.float32) / np.float32(128)))
    ang = np.arange(4096, dtype=np.float32)[:, None] * inv[None, :].astype(np.float32)
    return np.concatenate([np.cos(ang), np.sin(ang)], axis=1).astype(np.float32)


_NC_CACHE = {}


def _weights_map(pre_norm_g, w_in, attn_sink, gmlp_ln_g, gmlp_ln_b, gmlp_ws, gmlp_bs, w_out, post_norm_g, w_pe, w_pg):
    c = np.ascontiguousarray
    return {
        "w_in": c(w_in[0]), "w_out": c(w_out[0]), "w_pg": c(w_pg[0]), "w_pe": c(w_pe[0]),
        "pre_g": c(pre_norm_g[0:1]), "post_g": c(post_norm_g[0:1]), "sink": c(attn_sink[0:1]),
        "ln_g": c(gmlp_ln_g[0:1]), "ln_b": c(gmlp_ln_b[0:1]), "ws": c(gmlp_ws[0]),
        "bs": c(gmlp_bs[0].reshape(1, 1024)), "cs": _cs_table(),
    }


def kernel(x_prompt, x_sample, p_prompt, p_sample, pre_norm_g, w_in, attn_sink, gmlp_ln_g, gmlp_ln_b,
           gmlp_ws, gmlp_bs, w_out, post_norm_g, w_pe, w_pg):
    f = lambda a: np.asarray(a, dtype=np.float32)
    x_prompt, x_sample, p_prompt, p_sample = f(x_prompt), f(x_sample), f(p_prompt), f(p_sample)
    wm = _weights_map(f(pre_norm_g), f(w_in), f(attn_sink), f(gmlp_ln_g), f(gmlp_ln_b), f(gmlp_ws), f(gmlp_bs),
                      f(w_out), f(post_norm_g), f(w_pe), f(w_pg))
    n = 8
    seqs = [("a", 0, 32), ("b", 0, 16), ("b", 2048, 16)]
    key = "full"
    if key not in _NC_CACHE:
        _NC_CACHE[key] = build(seqs, 4096, 4096)
    nc = _NC_CACHE[key]
    in_maps = []
    for c in range(n):
        m = dict(wm)
        m["xa"] = np.ascontiguousarray(x_prompt[c])
        m["xb"] = np.ascontiguousarray(x_sample[2 * c:2 * c + 2].reshape(4096, D))
        m["pa"] = np.ascontiguousarray(p_prompt[0, c])
        m["pb"] = np.ascontiguousarray(p_sample[0, 2 * c:2 * c + 2].reshape(4096, 256))
        in_maps.append(m)
    res = run_bass_kernel_spmd(nc, in_maps, core_ids=list(range(n)))
    y_prompt = np.stack([np.asarray(r["ya"], dtype=np.float32) for r in res.results], axis=0)
    y_sample = np.concatenate([np.asarray(r["yb"], dtype=np.float32).reshape(2, 2048, D) for r in res.results], axis=0)
    return (y_prompt, y_sample)
```

```python
import math
from contextlib import ExitStack

import numpy as np
import concourse.bass as bass
import concourse.mybir as mybir
from concourse.bass_utils import run_bass_kernel_spmd

F32 = mybir.dt.float32
BF16 = mybir.dt.bfloat16
AF = mybir.ActivationFunctionType
ALU = mybir.AluOpType
AX = mybir.AxisListType

D = 2048
INW = 5632
NB = 4
EPS = 1e-6
SCALE = 128.0 ** -0.5


class _Op:
    __slots__ = ("eng", "fn", "deps", "dma", "key", "sig", "cnt")

    def __init__(self, eng, fn, deps, key):
        self.eng = eng
        self.fn = fn
        self.deps = deps
        self.dma = key is not None
        self.key = key
        self.sig = False
        self.cnt = 0


class Sched:
    ENGS = ("pe", "act", "dve", "pool", "sp")

    def __init__(self):
        self.ops = []
        self.lastw = {}
        self.readers = {}
        self.group_keys = set()

    def add(self, eng, fn, reads=(), writes=(), dma_key=None):
        idx = len(self.ops)
        deps = set()
        for r in reads:
            w = self.lastw.get(r)
            if w is not None:
                deps.add(w)
        for w_ in writes:
            w = self.lastw.get(w_)
            if w is not None:
                deps.add(w)
            rl = self.readers.get(w_)
            if rl:
                deps.update(rl)
        for r in reads:
            self.readers.setdefault(r, []).append(idx)
        for w_ in writes:
            self.lastw[w_] = idx
            self.readers[w_] = []
        deps.discard(idx)
        self.ops.append(_Op(eng, fn, deps, dma_key))
        return idx

    def emit(self, nc, stack):
        ops = self.ops
        for op in ops:
            nd = set()
            for d in op.deps:
                p = ops[d]
                if (not p.dma) and p.eng == op.eng and p.eng in ("pe", "sp"):
                    continue
                if p.dma and op.dma and p.key == op.key and p.key in self.group_keys:
                    continue
                nd.add(d)
            op.deps = nd
            for d in nd:
                ops[d].sig = True
        esem = {e: stack.enter_context(nc.semaphore("s_" + e)) for e in self.ENGS}
        dsem, dcount = {}, {}
        ecount = {e: 0 for e in self.ENGS}
        for op in ops:
            if op.dma:
                op.sig = True
                if op.key not in dsem:
                    dsem[op.key] = stack.enter_context(nc.semaphore("d%d" % len(dsem)))
                    dcount[op.key] = 0
                dcount[op.key] += 16
                op.cnt = dcount[op.key]
            elif op.sig:
                ecount[op.eng] += 1
                op.cnt = ecount[op.eng]
        per_eng = {e: [] for e in self.ENGS}
        for op in ops:
            per_eng[op.eng].append(op)
        block = stack.enter_context(nc.Block())
        group_keys = self.group_keys

        def run(e, engine):
            waited = {}
            for op in per_eng[e]:
                need = {}
                for d in op.deps:
                    p = ops[d]
                    if p.dma:
                        s = dsem[p.key]
                        v = dcount[p.key] if p.key in group_keys else p.cnt
                    else:
                        s = esem[p.eng]
                        v = p.cnt
                    k = id(s)
                    if k not in need or need[k][1] < v:
                        need[k] = (s, v)
                for k, (s, v) in need.items():
                    if waited.get(k, 0) >= v:
                        continue
                    waited[k] = v
                    engine.wait_ge(s, v)
                ins = op.fn(engine)
                if op.sig:
                    if op.dma:
                        ins.then_inc(dsem[op.key], 16)
                    else:
                        ins.then_inc(esem[op.eng], 1)
            if e == "sp":
                for k, s in dsem.items():
                    engine.wait_ge(s, dcount[k])

        @block.tensor
        def _(eng):
            run("pe", eng)

        @block.scalar
        def _(eng):
            run("act", eng)

        @block.vector
        def _(eng):
            run("dve", eng)

        @block.gpsimd
        def _(eng):
            run("pool", eng)

        @block.sync
        def _(eng):
            run("sp", eng)


def build(seqs, rows_a, rows_b, max_ops=None, marks=None):
    nc = bass.Bass("TRN2", target_bir_lowering=False)

    def din(name, shape):
        return nc.dram_tensor(name, shape, F32, kind="ExternalInput").ap()

    xin = {"a": din("xa", [rows_a, D]), "b": din("xb", [rows_b, D])}
    pin = {"a": din("pa", [rows_a, 256]), "b": din("pb", [rows_b, 256])}
    yout = {"a": nc.dram_tensor("ya", [rows_a, D], F32, kind="ExternalOutput").ap(),
            "b": nc.dram_tensor("yb", [rows_b, D], F32, kind="ExternalOutput").ap()}
    w_in = din("w_in", [D, INW])
    w_out = din("w_out", [D, D])
    w_pg = din("w_pg", [D, D])
    w_pe = din("w_pe", [256, D])
    pre_g = din("pre_g", [1, D])
    post_g = din("post_g", [1, D])
    sink = din("sink", [1, 8])
    ln_g = din("ln_g", [1, 1024])
    ln_b = din("ln_b", [1, 1024])
    ws_d = din("ws", [8, 128, 128])
    bs_d = din("bs", [1, 1024])
    cs_d = din("cs", [4096, 128])
    win_s = nc.dram_tensor("win_s", [11, 128, 8192], BF16).ap()
    wout_s = nc.dram_tensor("wout_s", [4, 128, 8192], BF16).ap()
    wpg_s = nc.dram_tensor("wpg_s", [4, 128, 8192], BF16).ap()
    wpe_s = nc.dram_tensor("wpe_s", [4, 128, 1024], BF16).ap()

    S = Sched()
    S.group_keys.add("setup")
    with ExitStack() as st:
        st.enter_context(nc.allow_non_contiguous_dma(reason="tiny per-partition parameter loads"))

        def sb(name, shape, dt):
            return st.enter_context(nc.sbuf_tensor("sb_" + name, shape, dt))

        RBt = sb("RB", [128, 12288], BF16)
        hT = RBt[:, :].rearrange("p (k t) -> p k t", k=16)
        hb = sb("hb", [128, 2, 2048], BF16)
        _rbf = RBt[:, :].bitcast(F32)
        xrl = [_rbf[:, 0:2048], _rbf[:, 2048:4096], _rbf[:, 4096:6144],
               hb[:, :, :].rearrange("p s c -> p (s c)").bitcast(F32)]
        x1b = sb("x1b", [128, 2, 2048], BF16)
        XHt = sb("XH", [128, 4096], F32)
        xh = XHt[:, :].rearrange("p (s c) -> p s c", s=2)
        gv = XHt[:, :].rearrange("p (b c) -> p b c", b=4)
        amT = XHt[:, :].bitcast(BF16).rearrange("p (k t) -> p k t", k=16)
        ys = sb("ys", [128, 4, 2048], F32)
        QTt = sb("QT", [128, 4096], BF16)
        qT = QTt[:, :].rearrange("p (h t) -> p h t", h=8)
        pf = QTt[:, 0:2048].bitcast(F32).rearrange("p (b c) -> p b c", b=4)
        pbb = QTt[:, 2048:3072].rearrange("p (b c) -> p b c", b=4)
        pT = QTt[:, 3072:4096].rearrange("p (c t) -> p c t", c=2)
        kT = sb("kT", [128, 2, 768], BF16)
        vv = sb("vv", [128, 6, 256], BF16)
        GUt = sb("GU", [128, 8192], BF16)
        gaT = GUt[:, 0:4096].rearrange("p (h t) -> p h t", h=8)
        uT = GUt[:, 4096:8192].rearrange("p (h t) -> p h t", h=8)
        x1T = GUt[:, :].rearrange("p (k t) -> p k t", k=16)
        NNt = sb("NN", [128, 4096], BF16)
        nn = NNt[:, :].rearrange("p (b c) -> p b c", b=4)
        sig = NNt[:, 0:2048].bitcast(F32).rearrange("p (s c) -> p s c", s=2)
        tmp2 = NNt[:, 2048:4096].bitcast(F32).rearrange("p (s c) -> p s c", s=2)
        PT = sb("PT", [128, 2, 3, 512], BF16)
        RTt = sb("RT", [128, 2048], F32)
        Rr = RTt[:, 0:1024].rearrange("p (s c) -> p s c", s=2)
        tmpA = RTt[:, 1024:2048].rearrange("p (s c) -> p s c", s=2)
        g_bc = RTt[:, :]
        sg = sb("sg", [128, 2, 512], F32)
        ropeA = sb("ropeA", [128, 512], F32)
        ropeB = sb("ropeB", [128, 512], F32)
        qr = sb("qr", [128, 2, 512], BF16)
        junk = sb("junk", [128, 1024], BF16)
        postg = sb("postg", [128, 2048], F32)
        cs = sb("cs", [128, 6, 128], F32)
        Bias = sb("Bias", [128, 8, 128], F32)
        wsT = sb("wsT", [128, 8, 128], BF16)
        ident = sb("ident", [128, 128], BF16)
        ones = sb("ones", [128, 128], BF16)
        maskP = sb("maskP", [128, 128], BF16)
        maskN = sb("maskN", [128, 128], BF16)
        mf = sb("mf", [128, 128], F32)
        esink = sb("esink", [128, 8], F32)
        lng = sb("lng", [128, 8], F32)
        lnb = sb("lnb", [128, 8], F32)
        gk = sb("gk", [128, 16], F32)
        stt = sb("stt", [128, 64], F32)
        negh = sb("negh", [128, 4], F32)
        wr = sb("wr", [128, 2, 8192], BF16)
        wpe = sb("wpe", [128, 2, 1024], BF16)
        psum = [st.enter_context(nc.psum_tensor("ps%d" % i, [128, 512], F32)) for i in range(8)]
        psb = [p[:, :].bitcast(BF16) for p in psum]
        bank_ctr = [0]

        def bank():
            k = bank_ctr[0] % 8
            bank_ctr[0] += 1
            return k

        phase_cur = {}

        def phase(tok, ph):
            if phase_cur.get(tok) != ph:
                phase_cur[tok] = ph
                return [], [tok]
            return [tok], []

        C_SSQX, C_RX = 0, 6
        C_GSUM, C_GSSQ, C_GMEAN, C_GMSQ, C_GR = 12, 20, 24, 28, 32
        C_YSSQ, C_YT, C_YR = 36, 52, 56
        C_EPS = 60
        C_NH = 61

        def col(c):
            return stt[:, c:c + 1]

        def setup_dma(out, in_, w):
            S.add("sp", lambda e: e.dma_start(out=out, in_=in_), writes=[w], dma_key="setup")

        setup_dma(postg[:, :], post_g.partition_broadcast(128), "postg")
        setup_dma(esink[:, :], sink.partition_broadcast(128), "esink")
        bs_bc = tmpA[:, :, :].rearrange("p s c -> p (s c)")
        setup_dma(bs_bc, bs_d.partition_broadcast(128), ("tmpA", 0))
        for kc in range(16):
            setup_dma(gk[:, kc:kc + 1], pre_g[0:1, kc * 128:(kc + 1) * 128].rearrange("o p -> p o"), "gk")
        for h in range(8):
            setup_dma(lng[:, h:h + 1], ln_g[0:1, h * 128:(h + 1) * 128].rearrange("o p -> p o"), "lng")
            setup_dma(lnb[:, h:h + 1], ln_b[0:1, h * 128:(h + 1) * 128].rearrange("o p -> p o"), "lnb")
        wsf = sg[:, :, :].rearrange("p s (h q) -> p (s h) q", h=4)
        setup_dma(wsf, ws_d.rearrange("h p q -> p h q"), ("sg", 0))
        S.add("pool", lambda e: e.memset(mf[:, :], 1.0), writes=["mf"])
        S.add("pool", lambda e: e.affine_select(out=mf[:, :], in_=mf[:, :], pattern=[[-1, 128]], compare_op=ALU.is_equal,
                                                fill=0.0, base=0, channel_multiplier=1), reads=["mf"], writes=["mf"])
        S.add("pool", lambda e: e.tensor_copy(out=ident[:, :], in_=mf[:, :]), reads=["mf"], writes=["ident"])
        S.add("pool", lambda e: e.memset(mf[:, :], 1.0), reads=["mf"], writes=["mf"])
        S.add("pool", lambda e: e.affine_select(out=mf[:, :], in_=mf[:, :], pattern=[[-1, 128]], compare_op=ALU.is_ge,
                                                fill=0.0, base=0, channel_multiplier=1), reads=["mf"], writes=["mf"])
        S.add("pool", lambda e: e.tensor_copy(out=maskP[:, :], in_=mf[:, :]), reads=["mf"], writes=["maskP"])
        S.add("pool", lambda e: e.memset(mf[:, :], 1.0), reads=["mf"], writes=["mf"])
        S.add("pool", lambda e: e.affine_select(out=mf[:, :], in_=mf[:, :], pattern=[[1, 128]], compare_op=ALU.is_ge,
                                                fill=0.0, base=0, channel_multiplier=-1), reads=["mf"], writes=["mf"])
        S.add("pool", lambda e: e.tensor_copy(out=maskN[:, :], in_=mf[:, :]), reads=["mf"], writes=["maskN"])
        S.add("pool", lambda e: e.memset(ones[:, :], 1.0), writes=["ones"])
        S.add("pool", lambda e: e.memset(stt[:, C_EPS:C_EPS + 1], EPS), writes=["epsc"])
        S.add("pool", lambda e: e.memset(negh[:, :], -0.5), writes=["negh"])
        S.add("act", lambda e: e.activation(out=esink[:, :], in_=esink[:, :], func=AF.Exp), reads=["esink"], writes=["esink"])
        wsb = qr[:, :, :].rearrange("p s (h q) -> p (s h) q", h=4)
        S.add("dve", lambda e: e.tensor_copy(out=wsb, in_=wsf), reads=[("sg", 0)], writes=[("qr", 0)])
        k0 = bank()

        def f_wsT(e):
            for h in range(8):
                ins = e.transpose(out=psb[k0][:, h * 128:(h + 1) * 128], in_=wsb[:, h, :], identity=ident[:, :])
            return ins
        S.add("pe", f_wsT, reads=[("qr", 0), "ident"], writes=[("ps", k0)])
        S.add("act", lambda e: e.copy(out=wsT[:, :, :].rearrange("p h q -> p (h q)"), in_=psb[k0][:, :]),
              reads=[("ps", k0)], writes=["wsT"])
        for half in range(2):
            kk = bank()
            S.add("pe", lambda e, kk=kk, half=half: e.matmul(
                psum[kk][:, :], lhsT=ones[:, :], rhs=wsT[:, 4 * half:4 * half + 4, :].rearrange("p h q -> p (h q)"), start=True, stop=True),
                reads=["ones", "wsT"], writes=[("ps", kk)])
            for h4 in range(4):
                h = 4 * half + h4
                S.add("dve", lambda e, kk=kk, h=h, h4=h4: e.scalar_tensor_tensor(
                    out=Bias[:, h, :], in0=psum[kk][:, h4 * 128:(h4 + 1) * 128], scalar=lnb[:, h:h + 1],
                    in1=bs_bc[:, h * 128:(h + 1) * 128], op0=ALU.mult, op1=ALU.add),
                    reads=[("ps", kk), "lnb", ("tmpA", 0)], writes=[("Bias", h)])

        def mark(nm):
            if marks is not None:
                marks.append((nm, len(S.ops)))
        mark("setup_done")
        w_in_v = w_in.rearrange("(kc p) n -> p kc n", p=128)
        w_out_v = w_out.rearrange("(kc p) n -> p kc n", p=128)
        w_pg_v = w_pg.rearrange("(kc p) n -> p kc n", p=128)
        w_pe_v = w_pe.rearrange("(kc p) n -> p kc n", p=128)
        stF = [ys[:, 0, :], ys[:, 1, :], ys[:, 2, :]]
        _yb = ys[:, 3, :].bitcast(BF16)
        stB = [_yb[:, 0:2048], _yb[:, 2048:4096]]
        cj = [0, 0]
        conv_tok = {}

        def cv_engine_op(eng, o, i_, sc):
            if sc is None:
                if eng == "act":
                    return lambda e: e.copy(out=o, in_=i_)
                return lambda e: e.tensor_copy(out=o, in_=i_)

            def f(e):
                for kc in range(4):
                    oo, ii = o[:, kc * 512:(kc + 1) * 512], i_[:, kc * 512:(kc + 1) * 512]
                    if eng == "act":
                        ins = e.mul(out=oo, in_=ii, mul=sc[kc])
                    else:
                        ins = e.tensor_scalar(out=oo, in0=ii, scalar1=sc[kc], scalar2=None, op0=ALU.mult)
                return ins
            return f

        def conv_direct(p, slot):
            S.add("pool", lambda e: e.dma_start(out=wr[:, slot, :].rearrange("p (k c) -> p k c", k=16),
                                                in_=w_in_v[:, :, p * 512:(p + 1) * 512]),
                  writes=[("wr", slot)], dma_key=("wrc", slot))
            S.add("sp", lambda e: e.dma_start(out=win_s[p, :, :], in_=wr[:, slot, :]), reads=[("wr", slot)],
                  writes=[("win", p, "d")], dma_key=("wout_d", slot))
            conv_tok[("win", p)] = [("win", p, "d")]

        scratch_jobs = []
        for i in range(4):
            scratch_jobs.append((w_out_v[:, :, i * 512:(i + 1) * 512], wout_s[i, :, :], 16, ("wout", i)))
        for i in range(4):
            scratch_jobs.append((w_pe_v[:, :, i * 512:(i + 1) * 512], wpe_s[i, :, :], 2, ("wpe", i)))
            scratch_jobs.append((w_pg_v[:, :, i * 512:(i + 1) * 512], wpg_s[i, :, :], 16, ("wpg", i)))
        for (_, _, _, tok) in scratch_jobs:
            conv_tok[tok] = [tok + (0,)]
        for i in range(11):
            conv_tok[("win", i)] = [("win", i, "d")]

        def conv_scratch(n):
            for _ in range(n):
                if not scratch_jobs:
                    return
                src, dst, nk, tok = scratch_jobs.pop(0)
                S.add("pool", lambda e, src=src, dst=dst, nk=nk: e.dma_start(
                    out=dst.rearrange("p (k c) -> p k c", k=nk), in_=src),
                    writes=[tok + (0,)], dma_key=("cv",) + tok)

        first_use_extra = {"qr": [("qr", 0)], "sg1": [("sg", 0)], "tmpA1": [("tmpA", 0)]}

        tiles = []
        for (which, row0, nblk) in seqs:
            for b0 in range(0, nblk, NB):
                hl = 1 if b0 > 0 else 0
                hr = 1 if b0 + NB < nblk else 0
                tiles.append(dict(which=which, row0=row0, b0=b0, hl=hl, hr=hr, nkv=NB + hl + hr,
                                  prev_hl=(tiles[-1]["hl"] if hl else 0)))

        xh_ctr = [0]

        def stage0(T, as_list=False):
            out_list = []
            nkv, hl = T["nkv"], T["hl"]
            xsrc = xin[T["which"]]
            pos0 = (T["b0"] - hl) * 128
            def f_cs():
                S.add("sp", lambda e: e.dma_start(out=g_bc, in_=pre_g.partition_broadcast(128)),
                      writes=["gbc", ("Rr", 0), ("Rr", 1), ("tmpA", 0), ("tmpA", 1)], dma_key="gbc")
                S.add("sp", lambda e: e.dma_start(out=cs[:, 0:nkv, :],
                                                  in_=cs_d[pos0:pos0 + nkv * 128, :].rearrange("(s p) c -> p s c", p=128)),
                      writes=["cs"], dma_key="cs")
            out_list.append(f_cs)
            for s in range(hl, nkv):
                out_list.append(lambda s=s: s0_block(T, s, xsrc, pos0))
            if as_list:
                return out_list
            for f in out_list:
                pb_ = f()
                if pb_ is not None:
                    pb_()

        def s0_block(T, s, xsrc, pos0):
            if True:
                i = xh_ctr[0] % 2
                xh_ctr[0] += 1
                r0 = T["row0"] + pos0 + s * 128
                pr, pw = phase("XH", ("x", id(T)))
                S.add("sp", lambda e, i=i, r0=r0: e.dma_start(out=xh[:, i, :], in_=xsrc[r0:r0 + 128, :]),
                      reads=pr, writes=[("xh", i)] + pw + first_use_extra.pop("XH", []), dma_key=("xh", i))
                S.add("act", lambda e, i=i, s=s: e.activation(out=hb[:, i, :], in_=xh[:, i, :], func=AF.Square,
                                                              accum_out=col(C_SSQX + s)),
                      reads=[("xh", i), "XH"], writes=[("hb", i), ("st", C_SSQX + s)])
                S.add("pool", lambda e, s=s: e.tensor_scalar(out=col(C_RX + s), in0=col(C_SSQX + s), scalar1=1.0 / D,
                                                             scalar2=EPS, op0=ALU.mult, op1=ALU.add),
                      reads=[("st", C_SSQX + s)], writes=[("st", C_RX + s)])
                S.add("pool", lambda e, s=s: e.tensor_tensor(out=col(C_RX + s), in0=col(C_RX + s), in1=negh[:, 0:1], op=ALU.pow),
                      reads=[("st", C_RX + s), "negh"], writes=[("st", C_RX + s)])
                S.add("dve", lambda e, i=i, s=s: e.scalar_tensor_tensor(out=hb[:, i, :], in0=xh[:, i, :], scalar=col(C_RX + s),
                                                                        in1=g_bc, op0=ALU.mult, op1=ALU.mult),
                      reads=[("xh", i), ("st", C_RX + s), "XH", "gbc", ("Rr", 0), ("Rr", 1), ("tmpA", 0), ("tmpA", 1)],
                      writes=[("hb", i)])

            def partB(i=i, s=s):
                for half in range(2):
                    k = bank()

                    def f_tr(e, i=i, half=half, k=k):
                        for c in range(8):
                            ins = e.transpose(out=psb[k][:, c * 128:(c + 1) * 128],
                                              in_=hb[:, i, (half * 8 + c) * 128:(half * 8 + c + 1) * 128], identity=ident[:, :])
                        return ins
                    S.add("pe", f_tr, reads=[("hb", i), "ident"], writes=[("ps", k)])
                    pr, pw = phase("RB", ("h", id(T)))
                    eng = "act" if half == 0 else "dve"

                    def f_cp(e, k=k, half=half, s=s, eng=eng):
                        o = hT[:, half * 8:half * 8 + 8, s * 128:(s + 1) * 128]
                        i_ = psb[k][:, :].rearrange("p (c t) -> p c t", c=8)
                        return e.copy(out=o, in_=i_) if eng == "act" else e.tensor_copy(out=o, in_=i_)
                    S.add(eng, f_cp, reads=[("ps", k)] + pr, writes=[("hT", s)] + pw)
            return partB

        wr_ctr = [0]

        def load_piece(src, toks):
            s = wr_ctr[0] % 2
            wr_ctr[0] += 1
            S.add("sp", lambda e: e.dma_start(out=wr[:, s, :], in_=src), reads=toks, writes=[("wr", s)], dma_key=("wr", s))
            return s

        deferred = []

        def run_deferred():
            n = len(deferred)
            for _ in range(n):
                deferred.pop(0)()

        def mm_tok(T, ws, k, slot, ncols=512, c0=0):
            def f(e):
                for kc in range(16):
                    ins = e.matmul(psum[k][:, 0:ncols], lhsT=hT[:, kc, slot * 128:(slot + 1) * 128],
                                   rhs=wr[:, ws, kc * 512 + c0:kc * 512 + c0 + ncols], start=(kc == 0), stop=(kc == 15))
                return ins
            S.add("pe", f, reads=[("wr", ws), ("hT", slot), "RB"], writes=[("ps", k)])
            run_deferred()

        def mm_feat(T, ws, k, cg):
            hl = T["hl"]

            def f(e):
                for kc in range(16):
                    ins = e.matmul(psum[k][:, :], lhsT=wr[:, ws, kc * 512 + cg * 128:kc * 512 + (cg + 1) * 128],
                                   rhs=hT[:, kc, hl * 128:hl * 128 + 512], start=(kc == 0), stop=(kc == 15))
                return ins
            S.add("pe", f, reads=[("wr", ws), "RB"] + [("hT", hl + b) for b in range(NB)], writes=[("ps", k)])
            run_deferred()

        def rope_evac(k, nh, slot, dst_fn, dst_tok_r, dst_tok_w):
            w = nh * 128
            psv = psum[k][:, 0:w].rearrange("p (h t d) -> p h t d", h=nh, t=2)
            Av = ropeA[:, 0:w].rearrange("p (h t d) -> p h t d", h=nh, t=2)
            Bv = ropeB[:, 0:w].rearrange("p (h t d) -> p h t d", h=nh, t=2)
            qs = slot % 2
            qv = qr[:, qs, 0:w].rearrange("p (h t d) -> p h t d", h=nh, t=2)
            cosb = cs[:, slot, 0:64]
            sinb = cs[:, slot, 64:128]
            S.add("dve", lambda e: e.tensor_tensor(out=Av, in0=psv,
                                                   in1=cosb.unsqueeze(1).unsqueeze(1).to_broadcast([128, nh, 2, 64]), op=ALU.mult),
                  reads=[("ps", k), "cs"], writes=["ropeA"])
            S.add("dve", lambda e: e.tensor_tensor(out=Bv[:, :, 0, :], in0=psv[:, :, 1, :],
                                                   in1=sinb.unsqueeze(1).to_broadcast([128, nh, 64]), op=ALU.mult),
                  reads=[("ps", k), "cs"], writes=["ropeB0"])
            S.add("dve", lambda e: e.tensor_tensor(out=Bv[:, :, 1, :], in0=psv[:, :, 0, :],
                                                   in1=sinb.unsqueeze(1).to_broadcast([128, nh, 64]), op=ALU.mult),
                  reads=[("ps", k), "cs"], writes=["ropeB1"])
            S.add("pool", lambda e: e.tensor_tensor(out=qv[:, :, 0, :], in0=Av[:, :, 0, :], in1=Bv[:, :, 0, :], op=ALU.subtract),
                  reads=["ropeA", "ropeB0"], writes=[("qr", qs, 0)] + first_use_extra.pop("qr", []))
            S.add("pool", lambda e: e.tensor_tensor(out=qv[:, :, 1, :], in0=Av[:, :, 1, :], in1=Bv[:, :, 1, :], op=ALU.add),
                  reads=["ropeA", "ropeB1"], writes=[("qr", qs, 1)])
            def part2():
                k2 = bank()

                def f_tr(e):
                    for h in range(nh):
                        ins = e.transpose(out=psb[k2][:, h * 128:(h + 1) * 128], in_=qr[:, qs, h * 128:(h + 1) * 128],
                                          identity=ident[:, :])
                    return ins
                S.add("pe", f_tr, reads=[("qr", qs, 0), ("qr", qs, 1), ("qr", 0), "ident"], writes=[("ps", k2)])
                S.add("act", lambda e: e.copy(out=dst_fn(), in_=psb[k2][:, 0:w].rearrange("p (h t) -> p h t", h=nh)),
                      reads=[("ps", k2)] + dst_tok_r, writes=dst_tok_w)
            deferred.append(part2)

        def piece_kv(T, ws):
            if T["hl"]:
                ps_ = T["prev_hl"] + NB - 1
                for d_, s_ in ((0, ps_), (1, ps_ + 1)):
                    S.add("pool", lambda e, d_=d_, s_=s_: e.tensor_copy(out=kT[:, :, d_ * 128:(d_ + 1) * 128],
                                                                        in_=kT[:, :, s_ * 128:(s_ + 1) * 128]),
                          reads=[("kT", s_)], writes=[("kT", d_)])
                    S.add("pool", lambda e, d_=d_, s_=s_: e.tensor_copy(out=vv[:, d_, :], in_=vv[:, s_, :]),
                          reads=[("vv", s_)], writes=[("vv", d_)])
            for s in range(2 * T["hl"], T["nkv"]):
                k = bank()
                mm_tok(T, ws, k, s)
                rope_evac(k, 2, s, lambda s=s: kT[:, :, s * 128:(s + 1) * 128], [], [("kT", s)])
                S.add("dve", lambda e, k=k, s=s: e.tensor_copy(out=vv[:, s, :], in_=psum[k][:, 256:512]),
                      reads=[("ps", k)], writes=[("vv", s)])

        def piece_q(T, ws, g):
            for b in range(NB):
                k = bank()
                mm_tok(T, ws, k, T["hl"] + b)
                pr, pw = phase("QT", ("q", id(T)))
                rope_evac(k, 4, T["hl"] + b, lambda b=b: qT[:, 4 * g:4 * g + 4, b * 128:(b + 1) * 128], pr, [("qT", g, b)] + pw)

        def piece_gate(T, ws, which, half, hooks=None):
            for cg in range(4):
                if hooks and cg > 0 and hooks[cg - 1] is not None:
                    hooks[cg - 1]()
                k = bank()
                mm_feat(T, ws, k, cg)
                h = 4 * half + cg
                if which == "ga":
                    pr, pw = phase("GU", ("g", id(T)))
                    ex = first_use_extra.pop("GU", [])
                    S.add("act", lambda e, k=k, h=h: e.activation(out=gaT[:, h, :], in_=psum[k][:, :], func=AF.Silu),
                          reads=[("ps", k)] + pr, writes=[("gaT", h)] + pw + ex)
                elif which == "u":
                    pr, pw = phase("GU", ("g", id(T)))
                    S.add("act", lambda e, k=k, h=h: e.activation(out=uT[:, h, :], in_=psum[k][:, :], func=AF.Gelu),
                          reads=[("ps", k)] + pr, writes=[("uT", h)] + pw)
                else:
                    s2 = cg % 2
                    S.add("act", lambda e, k=k, s2=s2: e.activation(out=sg[:, s2, :], in_=psum[k][:, :], func=AF.Silu),
                          reads=[("ps", k)], writes=[("sg", s2)] + (first_use_extra.pop("sg1", []) if s2 == 1 else []))
                    S.add("dve", lambda e, h=h, s2=s2: e.tensor_tensor(out=uT[:, h, :], in0=uT[:, h, :], in1=sg[:, s2, :],
                                                                       op=ALU.mult),
                          reads=[("sg", s2), ("uT", h), "GU"], writes=[("uT", h)])
            if hooks and hooks[3] is not None:
                hooks[3]()

        def piece_vg(T, ws, half):
            for b in range(NB):
                k = bank()
                mm_tok(T, ws, k, T["hl"] + b)
                pr, pw = phase("XH", ("gv", id(T)))
                S.add("act", lambda e, k=k, b=b: e.activation(out=gv[:, b, half * 512:(half + 1) * 512], in_=psum[k][:, :],
                                                              func=AF.Gelu, accum_out=col(C_GSUM + 2 * b + half)),
                      reads=[("ps", k)] + pr, writes=[("gv", b, half), ("st", C_GSUM + 2 * b + half)] + pw)
                if half == 1:
                    S.add("act", lambda e, b=b: e.activation(out=junk[:, :], in_=gv[:, b, :], func=AF.Square,
                                                             accum_out=col(C_GSSQ + b)),
                          reads=[("gv", b, 0), ("gv", b, 1), "XH"], writes=[("st", C_GSSQ + b), ("junk", 0), ("junk", 1)])
            if half == 1:
                gs = stt[:, C_GSUM:C_GSUM + 8].rearrange("p (b t) -> p b t", t=2)
                mean = stt[:, C_GMEAN:C_GMEAN + 4]
                msq = stt[:, C_GMSQ:C_GMSQ + 4]
                gr = stt[:, C_GR:C_GR + 4]
                gssq = stt[:, C_GSSQ:C_GSSQ + 4]
                rs = [("st", C_GSUM + i) for i in range(8)]
                S.add("dve", lambda e: e.tensor_tensor(out=mean, in0=gs[:, :, 0], in1=gs[:, :, 1], op=ALU.add),
                      reads=rs, writes=["gmean"])
                S.add("dve", lambda e: e.tensor_scalar(out=mean, in0=mean, scalar1=1.0 / 1024, scalar2=None, op0=ALU.mult),
                      reads=["gmean"], writes=["gmean"])
                S.add("dve", lambda e: e.tensor_tensor(out=msq, in0=mean, in1=mean, op=ALU.mult), reads=["gmean"], writes=["gmsq"])
                S.add("dve", lambda e: e.scalar_tensor_tensor(out=gr, in0=gssq, scalar=1.0 / 1024, in1=msq, op0=ALU.mult,
                                                              op1=ALU.subtract),
                      reads=["gmsq"] + [("st", C_GSSQ + b) for b in range(4)], writes=["gr"])
                S.add("dve", lambda e: e.tensor_scalar(out=gr, in0=gr, scalar1=EPS, scalar2=None, op0=ALU.add),
                      reads=["gr"], writes=["gr"])
                S.add("pool", lambda e: e.tensor_tensor(out=gr, in0=gr, in1=negh[:, 0:4], op=ALU.pow),
                      reads=["gr", "negh"], writes=["gr"])
                for b in range(NB):
                    pr, pw = phase("NN", ("n", id(T)))
                    S.add("dve", lambda e, b=b: e.tensor_scalar(out=nn[:, b, :], in0=gv[:, b, :],
                                                                 scalar1=stt[:, C_GMEAN + b:C_GMEAN + b + 1],
                                                                 scalar2=stt[:, C_GR + b:C_GR + b + 1],
                                                                 op0=ALU.subtract, op1=ALU.mult),
                          reads=[("gv", b, 0), ("gv", b, 1), "gmean", "gr", "XH"] + pr,
                          writes=[("nn", b)] + pw + first_use_extra.pop("NN", []))

        def att_units(T):
            nblk_seq = None
            return [(n, g) for n in range(NB) for g in range(2)]

        att_state = {}

        def att_qk(T, u, ui):
            n, g = u
            hl, nkv = T["hl"], T["nkv"]
            own = hl + n
            js = [s for s in (own - 1, own, own + 1) if 0 <= s < nkv]
            ps_ = ui % 2
            banks = []
            for jj, s in enumerate(js):
                k = bank()
                banks.append(k)
                S.add("pe", lambda e, k=k, s=s: e.matmul(psum[k][:, :].rearrange("p (h q) -> p h q", h=4), lhsT=kT[:, g, s * 128:(s + 1) * 128],
                                                         rhs=qT[:, 4 * g:4 * g + 4, n * 128:(n + 1) * 128], start=True, stop=True),
                      reads=[("kT", s), ("qT", g, n), "QT"], writes=[("ps", k)])
                S.add("act", lambda e, k=k, jj=jj: e.activation(out=PT[:, ps_, jj, :], in_=psum[k][:, :], func=AF.Exp, scale=SCALE),
                      reads=[("ps", k)], writes=[("PT", ps_, jj)])
                if s != own:
                    m = maskP if s < own else maskN
                    S.add("pool", lambda e, jj=jj, m=m: e.tensor_tensor(
                        out=PT[:, ps_, jj, :].rearrange("p (h q) -> p h q", h=4),
                        in0=PT[:, ps_, jj, :].rearrange("p (h q) -> p h q", h=4),
                        in1=m[:, :].unsqueeze(1).to_broadcast([128, 4, 128]), op=ALU.mult),
                        reads=[("PT", ps_, jj), "maskP", "maskN"], writes=[("PT", ps_, jj)])
            att_state[ui] = js

        def att_pv(T, u, ui):
            n, g = u
            js = att_state.pop(ui)
            ps_ = ui % 2
            kO, kD = bank(), bank()

            def f_pv(e):
                for jj, s in enumerate(js):
                    ins = e.matmul(psum[kO][:, :], lhsT=vv[:, s, g * 128:(g + 1) * 128], rhs=PT[:, ps_, jj, :],
                                   start=(jj == 0), stop=(jj == len(js) - 1))
                return ins

            def f_d(e):
                for jj, s in enumerate(js):
                    ins = e.matmul(psum[kD][:, :], lhsT=ones[:, :], rhs=PT[:, ps_, jj, :],
                                   start=(jj == 0), stop=(jj == len(js) - 1))
                return ins
            ptr = [("PT", ps_, jj) for jj in range(len(js))]
            S.add("pe", f_pv, reads=ptr + [("vv", s) for s in js], writes=[("ps", kO)])
            S.add("pe", f_d, reads=ptr + ["ones"], writes=[("ps", kD)])
            rs = ui % 2

            def f_r(e):
                for h in range(4):
                    ins = e.tensor_scalar(out=Rr[:, rs, h * 128:(h + 1) * 128], in0=psum[kD][:, h * 128:(h + 1) * 128],
                                          scalar1=esink[:, 4 * g + h:4 * g + h + 1], scalar2=None, op0=ALU.add)
                return ins
            S.add("dve", f_r, reads=[("ps", kD), "esink"], writes=[("Rr", rs)])
            S.add("dve", lambda e: e.reciprocal(out=Rr[:, rs, :], in_=Rr[:, rs, :]), reads=[("Rr", rs)], writes=[("Rr", rs)])
            S.add("dve", lambda e: e.tensor_tensor(out=tmpA[:, rs, :], in0=psum[kO][:, :], in1=Rr[:, rs, :], op=ALU.mult),
                  reads=[("ps", kO), ("Rr", rs)], writes=[("tmpA", rs)] + (first_use_extra.pop("tmpA1", []) if rs == 1 else []))
            pr, pw = phase("XH", ("am", id(T)))
            S.add("dve", lambda e: e.tensor_tensor(out=amT[:, 4 * g:4 * g + 4, n * 128:(n + 1) * 128],
                                                    in0=tmpA[:, rs, :].rearrange("p (h q) -> p h q", h=4),
                                                    in1=gaT[:, 4 * g:4 * g + 4, n * 128:(n + 1) * 128], op=ALU.mult),
                  reads=[("tmpA", rs), "GU"] + [("gaT", 4 * g + h) for h in range(4)] + pr,
                  writes=[("amT", 0, n, g)] + pw)

        sp_ctr = [0]

        def spatial(T, b, hq):
            k = bank()
            rs = sp_ctr[0] % 2
            sp_ctr[0] += 1

            def f(e):
                for h in range(4):
                    hh = 4 * hq + h
                    ins = e.matmul(psum[k][:, h * 128:(h + 1) * 128], lhsT=nn[:, b, hh * 128:(hh + 1) * 128],
                                   rhs=wsT[:, hh, :], start=True, stop=True)
                return ins
            S.add("pe", f, reads=[("nn", b), "wsT", "NN"], writes=[("ps", k)])

            def f2(e):
                for h in range(4):
                    hh = 4 * hq + h
                    ins = e.scalar_tensor_tensor(out=tmpA[:, rs, h * 128:(h + 1) * 128], in0=psum[k][:, h * 128:(h + 1) * 128],
                                                 scalar=lng[:, hh:hh + 1], in1=Bias[:, hh, :], op0=ALU.mult, op1=ALU.add)
                return ins
            S.add("dve", f2, reads=[("ps", k), "lng"] + [("Bias", 4 * hq + h) for h in range(4)],
                  writes=[("tmpA", rs)] + (first_use_extra.pop("tmpA1", []) if rs == 1 else []))
            pr, pw = phase("XH", ("am", id(T)))
            S.add("dve", lambda e: e.tensor_tensor(out=amT[:, 8 + 4 * hq:8 + 4 * hq + 4, b * 128:(b + 1) * 128],
                                                    in0=tmpA[:, rs, :].rearrange("p (h q) -> p h q", h=4),
                                                    in1=uT[:, 4 * hq:4 * hq + 4, b * 128:(b + 1) * 128], op=ALU.mult),
                  reads=[("tmpA", rs), "GU"] + [("uT", 4 * hq + h) for h in range(4)] + pr,
                  writes=[("amT", 1, b, hq)] + pw)

        def piece_wout(T, ws, i):
            if i == 0:
                x_reload(T)
            if i == 0:
                p_prep(T)
            if i == 3:
                p_prep_b(T)
            if i == 2:
                T["_wps"][0] = load_wpe(0)
            for b in range(NB):
                k = bank()

                def f(e, k=k, b=b):
                    for fc in range(16):
                        ins = e.matmul(psum[k][:, :], lhsT=amT[:, fc, b * 128:(b + 1) * 128],
                                       rhs=wr[:, ws, fc * 512:(fc + 1) * 512], start=(fc == 0), stop=(fc == 15))
                    return ins
                S.add("pe", f, reads=[("wr", ws), "XH"] + [("amT", 0, b, g) for g in range(2)] + [("amT", 1, b, g) for g in range(2)],
                      writes=[("ps", k)])
                ex = first_use_extra.pop("ys", [])
                S.add("act", lambda e, k=k, b=b: e.copy(out=ys[:, b, i * 512:(i + 1) * 512], in_=psum[k][:, :]),
                      reads=[("ps", k)], writes=[("ys", b, i)] + ex)
                js = b % 2
                S.add("act", lambda e, k=k, b=b, js=js: e.activation(out=junk[:, js * 512:(js + 1) * 512], in_=psum[k][:, :],
                                                                     func=AF.Square, accum_out=col(C_YSSQ + 4 * b + i)),
                      reads=[("ps", k)], writes=[("st", C_YSSQ + 4 * b + i), ("junk", js)])
                if i == 3:
                    if b >= 2:
                        x1_block_b(T, b - 2)
                    x1_block_a(T, b)
            if i == 3:
                T["_x1last"] = [lambda: x1_block_b(T, NB - 2), lambda: x1_block_b(T, NB - 1)]

        def x_reload(T):
            xsrc = xin[T["which"]]
            for b in range(NB):
                r0 = T["row0"] + (T["b0"] + b) * 128
                if b < 3:
                    pr, pw = phase("RB", ("xr", id(T)))
                    S.add("sp", lambda e, b=b, r0=r0: e.dma_start(out=xrl[b], in_=xsrc[r0:r0 + 128, :]), reads=pr,
                          writes=[("xrl", b)] + pw, dma_key=("xrl", b))
                else:
                    S.add("sp", lambda e, b=b, r0=r0: e.dma_start(out=xrl[b], in_=xsrc[r0:r0 + 128, :]),
                          writes=[("xrl", b), ("hb", 0), ("hb", 1)], dma_key=("xrl", b))

        def x1_block_a(T, b):
            ysb = [("ys", b, i_) for i_ in range(4)]
            yt, yr = col(C_YT + b), col(C_YR + b)
            c_ = [col(C_YSSQ + 4 * b + j) for j in range(4)]
            S.add("pool", lambda e: e.tensor_scalar(out=yt, in0=c_[0], scalar1=c_[1], scalar2=c_[2], op0=ALU.add, op1=ALU.add),
                  reads=[("st", C_YSSQ + 4 * b + j) for j in range(4)], writes=[("st", C_YT + b)])
            S.add("pool", lambda e: e.tensor_scalar(out=yt, in0=yt, scalar1=c_[3], scalar2=None, op0=ALU.add),
                  reads=[("st", C_YSSQ + 4 * b + 3), ("st", C_YT + b)], writes=[("st", C_YT + b)])
            S.add("pool", lambda e: e.tensor_scalar(out=yr, in0=yt, scalar1=1.0 / D, scalar2=EPS, op0=ALU.mult, op1=ALU.add),
                  reads=[("st", C_YT + b)], writes=[("st", C_YR + b)])
            S.add("pool", lambda e: e.tensor_tensor(out=yr, in0=yr, in1=negh[:, 0:1], op=ALU.pow),
                  reads=[("st", C_YR + b), "negh"], writes=[("st", C_YR + b)])
            S.add("dve", lambda e: e.scalar_tensor_tensor(out=ys[:, b, :], in0=ys[:, b, :], scalar=yr, in1=postg[:, :],
                                                          op0=ALU.mult, op1=ALU.mult),
                  reads=ysb + [("st", C_YR + b), "postg"], writes=ysb)
            S.add("dve", lambda e: e.tensor_tensor(out=ys[:, b, :], in0=ys[:, b, :], in1=xrl[b], op=ALU.add),
                  reads=ysb + [("xrl", b)] + (["RB"] if b < 3 else [("hb", 0), ("hb", 1)]), writes=ysb)
            S.add("act", lambda e: e.copy(out=x1b[:, b % 2, :], in_=ys[:, b, :]), reads=ysb, writes=[("x1b", b % 2)])

        def x1_block_b(T, b):
            for half in range(2):
                k = bank()

                def f_tr(e, half=half, k=k, b=b):
                    for c in range(8):
                        ins = e.transpose(out=psb[k][:, c * 128:(c + 1) * 128],
                                          in_=x1b[:, b % 2, (half * 8 + c) * 128:(half * 8 + c + 1) * 128],
                                          identity=ident[:, :])
                    return ins
                S.add("pe", f_tr, reads=[("x1b", b % 2), "ident"], writes=[("ps", k)])
                pr, pw = phase("GU", ("x", id(T)))

                eng = "act"

                def f_cp(e, k=k, half=half, b=b, eng=eng):
                    o = x1T[:, half * 8:half * 8 + 8, b * 128:(b + 1) * 128]
                    i_ = psb[k][:, :].rearrange("p (c t) -> p c t", c=8)
                    return e.copy(out=o, in_=i_) if eng == "act" else e.tensor_copy(out=o, in_=i_)
                S.add(eng, f_cp, reads=[("ps", k)] + pr, writes=[("x1T", b, half)] + pw)

        def p_prep(T):
            psrc = pin[T["which"]]
            r0 = T["row0"] + T["b0"] * 128
            pr, pw = phase("QT", ("p", id(T)))
            S.add("sp", lambda e: e.dma_start(out=pf, in_=psrc[r0:r0 + NB * 128, :].rearrange("(b p) c -> p b c", p=128)),
                  reads=pr, writes=["pf"] + pw, dma_key="pf")
            S.add("dve", lambda e: e.tensor_copy(out=pbb, in_=pf), reads=["pf", "QT"], writes=["pbb"])

        def p_prep_b(T):
            k = bank()

            def f_tr(e):
                for b in range(NB):
                    for c in range(2):
                        ins = e.transpose(out=psb[k][:, (c * 4 + b) * 128:(c * 4 + b + 1) * 128],
                                          in_=pbb[:, b, c * 128:(c + 1) * 128], identity=ident[:, :])
                return ins
            S.add("pe", f_tr, reads=["pbb", "ident", "QT"], writes=[("ps", k)])
            S.add("act", lambda e: e.copy(out=pT.rearrange("p c t -> p (c t)"), in_=psb[k][:, :]),
                  reads=[("ps", k), "QT"], writes=["pT"])

        wpe_ctr = [0]

        def load_wpe(i):
            s = wpe_ctr[0] % 2
            wpe_ctr[0] += 1
            S.add("sp", lambda e: e.dma_start(out=wpe[:, s, :], in_=wpe_s[i, :, :]), reads=conv_tok[("wpe", i)],
                  writes=[("wpe", s)], dma_key=("wpe", s))
            return s

        s5_ctr = [0]

        def piece_pg(T, ws, wps, i):
            yd = yout[T["which"]]
            for b in range(NB):
                kA, kB = bank(), bank()

                def fA(e, kA=kA, b=b):
                    for kc in range(16):
                        ins = e.matmul(psum[kA][:, :], lhsT=x1T[:, kc, b * 128:(b + 1) * 128],
                                       rhs=wr[:, ws, kc * 512:(kc + 1) * 512], start=(kc == 0), stop=(kc == 15))
                    return ins

                def fB(e, kB=kB, b=b):
                    for c in range(2):
                        ins = e.matmul(psum[kB][:, :], lhsT=pT[:, c, b * 128:(b + 1) * 128],
                                       rhs=wpe[:, wps, c * 512:(c + 1) * 512], start=(c == 0), stop=(c == 1))
                    return ins
                S.add("pe", fA, reads=[("wr", ws), ("x1T", b, 0), ("x1T", b, 1), "GU"], writes=[("ps", kA)])
                S.add("pe", fB, reads=[("wpe", wps), "pT", "QT"], writes=[("ps", kB)])
                s = s5_ctr[0] % 2
                s5_ctr[0] += 1
                pr, pw = phase("NN", ("s", id(T)))
                S.add("act", lambda e, kA=kA, s=s: e.activation(out=sig[:, s, :], in_=psum[kA][:, :], func=AF.Sigmoid),
                      reads=[("ps", kA)] + pr, writes=[("sig", s)] + pw)
                S.add("dve", lambda e, kB=kB, s=s: e.tensor_tensor(out=tmp2[:, s, :], in0=psum[kB][:, :], in1=sig[:, s, :],
                                                                  op=ALU.mult),
                      reads=[("ps", kB), ("sig", s), "NN"], writes=[("tmp2", s)])
                S.add("dve", lambda e, b=b, s=s: e.tensor_tensor(out=ys[:, b, i * 512:(i + 1) * 512],
                                                                 in0=ys[:, b, i * 512:(i + 1) * 512], in1=tmp2[:, s, :], op=ALU.add),
                      reads=[("tmp2", s), ("ys", b, i), "NN"], writes=[("ys", b, i)])
                gidx = i * NB + b
                if gidx <= 1 and T.get("_x1last"):
                    T["_x1last"].pop(0)()
                pend = T.setdefault("_s0pend", [])
                if gidx % 2 == 0:
                    if len(pend) >= 2:
                        pend.pop(0)()
                    if T.get("_s0next"):
                        if gidx == 0:
                            T["_s0next"].pop(0)()
                        if T["_s0next"]:
                            pend.append(T["_s0next"].pop(0)())
                if i == 3 and b == NB - 1:
                    while pend:
                        pend.pop(0)()
                    while T.get("_s0next"):
                        T["_s0next"].pop(0)()()
                if i == 3:
                    r0 = T["row0"] + (T["b0"] + b) * 128
                    S.add("sp", lambda e, b=b, r0=r0: e.dma_start(out=yd[r0:r0 + 128, :], in_=ys[:, b, :]),
                          reads=[("ys", b, i_) for i_ in range(4)], dma_key=("o", b))

        plan = []
        for ti, T in enumerate(tiles):
            T["_wps"] = {}

            def add_piece(src, toks, fn):
                plan.append((src, toks, fn))

            units = [(n, g) for g in range(2) for n in range(NB)]
            add_piece(win_s[2], conv_tok[("win", 2)], lambda ws, T=T: piece_kv(T, ws))
            add_piece(win_s[0], conv_tok[("win", 0)], lambda ws, T=T: piece_q(T, ws, 0))
            add_piece(win_s[1], conv_tok[("win", 1)], lambda ws, T=T: piece_q(T, ws, 1))
            add_piece(win_s[7], conv_tok[("win", 7)], lambda ws, T=T: piece_vg(T, ws, 0))
            add_piece(win_s[8], conv_tok[("win", 8)], lambda ws, T=T: piece_vg(T, ws, 1))

            def hk(pv_list, qk_list, T=T, units=units):
                h = [None, None, None, None]
                for j, ui in enumerate(pv_list):
                    h[2 * j] = (lambda ui=ui: att_pv(T, units[ui], ui))
                for j, ui in enumerate(qk_list):
                    h[2 * j + 1] = (lambda ui=ui: att_qk(T, units[ui], ui))
                return h
            add_piece(win_s[3], conv_tok[("win", 3)], lambda ws, T=T, hk=hk: piece_gate(T, ws, "ga", 0, hk([], [0, 1])))
            add_piece(win_s[4], conv_tok[("win", 4)], lambda ws, T=T, hk=hk: piece_gate(T, ws, "ga", 1, hk([0, 1], [2, 3])))
            add_piece(win_s[5], conv_tok[("win", 5)], lambda ws, T=T, hk=hk: piece_gate(T, ws, "u", 0, hk([2, 3], [4, 5])))
            add_piece(win_s[6], conv_tok[("win", 6)], lambda ws, T=T, hk=hk: piece_gate(T, ws, "u", 1, hk([4, 5], [6, 7])))
            add_piece(win_s[9], conv_tok[("win", 9)], lambda ws, T=T, hk=hk: piece_gate(T, ws, "gg", 0, hk([6, 7], [])))
            add_piece(win_s[10], conv_tok[("win", 10)], lambda ws, T=T: piece_gate(T, ws, "gg", 1))

            def p_sp(ws, T=T, units=units):
                for b in range(4):
                    for hq in range(2):
                        spatial(T, b, hq)
            add_piece(None, [], p_sp)
            for i in range(4):
                add_piece(wout_s[i], conv_tok[("wout", i)], lambda ws, T=T, i=i: piece_wout(T, ws, i))

            def p_x1(ws, T=T, ti=ti):

                T["_s0next"] = stage0(tiles[ti + 1], as_list=True) if ti + 1 < len(tiles) else []
            add_piece(None, [], p_x1)

            def p_pg(ws, T=T, i=0):
                if i + 1 < 4:
                    T["_wps"][i + 1] = load_wpe(i + 1)
                piece_pg(T, ws, T["_wps"][i], i)
            for i in range(4):
                add_piece(wpg_s[i], conv_tok[("wpg", i)], lambda ws, T=T, i=i, p_pg=p_pg: p_pg(ws, T, i))

        stage0(tiles[0])
        real = [pi for pi, p in enumerate(plan) if p[0] is not None]
        nxt_real = {}
        for a, b_ in zip(real[:-1], real[1:]):
            nxt_real[a] = b_
        loaded = {}

        n_t0 = len(plan) // len(tiles)
        win_order = [2, 0, 1, 7, 8, 3, 4, 5, 6, 9, 10]

        def ensure_loaded(pi):
            if pi not in loaded:
                if pi < 11:
                    slot = wr_ctr[0] % 2
                    wr_ctr[0] += 1
                    conv_direct(win_order[pi], slot)
                    loaded[pi] = slot
                else:
                    loaded[pi] = load_piece(plan[pi][0], plan[pi][1])

        mark("stage0_done")
        ensure_loaded(real[0])
        for pi, (src, toks, fn) in enumerate(plan):
            mark("plan%d" % pi)
            if src is not None:
                ensure_loaded(pi)
                if pi in nxt_real and nxt_real[pi] < 11:
                    ensure_loaded(nxt_real[pi])
                if pi < 11:
                    conv_scratch(2 if pi < 2 else 1)
                if pi in nxt_real:
                    ensure_loaded(nxt_real[pi])
                fn(loaded[pi])
            else:
                fn(None)

        if max_ops is not None:
            S.ops = S.ops[:max_ops]
        S.emit(nc, st)
    return nc


def _cs_table():
    inv = 1.0 / (10000.0 ** (np.arange(0, 128, 2, dtype=np.float32) / np.float32(128)))
    ang = np.arange(4096, dtype=np.float32)[:, None] * inv[None, :].astype(np.float32)
    return np.concatenate([np.cos(ang), np.sin(ang)], axis=1).astype(np.float32)


_NC_CACHE = {}


def _weights_map(pre_norm_g, w_in, attn_sink, gmlp_ln_g, gmlp_ln_b, gmlp_ws, gmlp_bs, w_out, post_norm_g, w_pe, w_pg):
    c = np.ascontiguousarray
    return {
        "w_in": c(w_in[0]), "w_out": c(w_out[0]), "w_pg": c(w_pg[0]), "w_pe": c(w_pe[0]),
        "pre_g": c(pre_norm_g[0:1]), "post_g": c(post_norm_g[0:1]), "sink": c(attn_sink[0:1]),
        "ln_g": c(gmlp_ln_g[0:1]), "ln_b": c(gmlp_ln_b[0:1]), "ws": c(gmlp_ws[0]),
        "bs": c(gmlp_bs[0].reshape(1, 1024)), "cs": _cs_table(),
    }


def kernel(x_prompt, x_sample, p_prompt, p_sample, pre_norm_g, w_in, attn_sink, gmlp_ln_g, gmlp_ln_b,
           gmlp_ws, gmlp_bs, w_out, post_norm_g, w_pe, w_pg):
    f = lambda a: np.asarray(a, dtype=np.float32)
    x_prompt, x_sample, p_prompt, p_sample = f(x_prompt), f(x_sample), f(p_prompt), f(p_sample)
    wm = _weights_map(f(pre_norm_g), f(w_in), f(attn_sink), f(gmlp_ln_g), f(gmlp_ln_b), f(gmlp_ws), f(gmlp_bs),
                      f(w_out), f(post_norm_g), f(w_pe), f(w_pg))
    n = 8
    seqs = [("a", 0, 32), ("b", 0, 16), ("b", 2048, 16)]
    key = "full"
    if key not in _NC_CACHE:
        _NC_CACHE[key] = build(seqs, 4096, 4096)
    nc = _NC_CACHE[key]
    in_maps = []
    for c in range(n):
        m = dict(wm)
        m["xa"] = np.ascontiguousarray(x_prompt[c])
        m["xb"] = np.ascontiguousarray(x_sample[2 * c:2 * c + 2].reshape(4096, D))
        m["pa"] = np.ascontiguousarray(p_prompt[0, c])
        m["pb"] = np.ascontiguousarray(p_sample[0, 2 * c:2 * c + 2].reshape(4096, 256))
        in_maps.append(m)
    res = run_bass_kernel_spmd(nc, in_maps, core_ids=list(range(n)))
    y_prompt = np.stack([np.asarray(r["ya"], dtype=np.float32) for r in res.results], axis=0)
    y_sample = np.concatenate([np.asarray(r["yb"], dtype=np.float32).reshape(2, 2048, D) for r in res.results], axis=0)
    return (y_prompt, y_sample)
```
